# Optimizing a Trainium2 kernel written in Bass

```python
import math
import jax, jax.numpy as jnp
from jax import lax
import numpy as np

D_MODEL = 1024
BATCH = 16
SEQ = 256
DEPTH = 1
DEC_BATCH = 2
DEC_SEQ = 2048
PAST_LEN = 256

GRID_W = 64
MIX_WIDTH = D_MODEL
ATT_WIDTH = MIX_WIDTH // 2
SSM_WIDTH = MIX_WIDTH - ATT_WIDTH
HEAD_DIM = 64
N_ATT_HEADS = ATT_WIDTH // (2 * HEAD_DIM)
SSM_GROUP = 16
N_SSM_GROUPS = SSM_WIDTH // SSM_GROUP
SSM_STATE = 64
IN_WIDTH = 3 * ATT_WIDTH + SSM_WIDTH
D_FF = -(-8 * D_MODEL // (3 * 256)) * 256
N_MOD = 6
Q_BLOCK = 128
ROPE_BASE = 10000.0
NORM_EPS = 1e-6

kernel_name = "hybrid_diffattn_s5_prefix_dit_step"


def rmsnorm(x, g):
    xf = x.astype(jnp.float32)
    y = xf * lax.rsqrt(jnp.mean(xf * xf, axis=-1, keepdims=True) + NORM_EPS)
    return (y * g.astype(jnp.float32)).astype(x.dtype)


def modulation(cond, w_mod, b_mod, dtype):
    m = jax.nn.silu(cond.astype(jnp.float32)) @ w_mod.astype(jnp.float32) + b_mod.astype(jnp.float32)
    return [t.astype(dtype) for t in jnp.split(m[:, None, :], N_MOD, axis=-1)]


def axial_rope_tables(n_rows):
    t = jnp.arange(n_rows * GRID_W)
    row = (t // GRID_W).astype(jnp.float32)
    col = (t % GRID_W).astype(jnp.float32)
    half = HEAD_DIM // 2
    inv_freq = ROPE_BASE ** (-jnp.arange(0, half, 2, dtype=jnp.float32) / half)
    ang_r = row[:, None] * inv_freq
    ang_c = col[:, None] * inv_freq
    ang = jnp.concatenate([ang_r, ang_r, ang_c, ang_c], axis=-1)
    return jnp.cos(ang), jnp.sin(ang)


def rotate_half_axial(x):
    x1, x2, x3, x4 = jnp.split(x, 4, axis=-1)
    return jnp.concatenate([-x2, x1, -x4, x3], axis=-1)


def apply_rope(x, cos, sin):
    c = cos[None, :, None, None, :]
    s = sin[None, :, None, None, :]
    xf = x.astype(jnp.float32)
    return (xf * c + rotate_half_axial(xf) * s).astype(x.dtype)


def diff_attention(q, k, v, lam, subln_g, lam_init):
    b, lq = q.shape[:2]
    nb = lq // Q_BLOCK
    scale = HEAD_DIM ** -0.5
    qb = q.reshape(b, nb, Q_BLOCK, N_ATT_HEADS, 2, HEAD_DIM).transpose(1, 0, 2, 3, 4, 5)
    vf = v.astype(jnp.float32)

    def one_block(q_blk):
        s = jnp.einsum('bqhmd,bkhmd->bhmqk', q_blk, k,
                       preferred_element_type=jnp.float32) * scale
        p = jax.nn.softmax(s, axis=-1)
        w = p[:, :, 0] - lam * p[:, :, 1]
        return jnp.einsum('bhqk,bkhe->bqhe', w, vf)

    o = lax.map(one_block, qb)
    o = o.transpose(1, 0, 2, 3, 4).reshape(b, lq, N_ATT_HEADS, 2 * HEAD_DIM)
    o = rmsnorm(o, subln_g) * (1.0 - lam_init)
    return o.reshape(b, lq, ATT_WIDTH)


def _complex_affine_combine(e1, e2):
    a1r, a1i, b1r, b1i = e1
    a2r, a2i, b2r, b2i = e2
    return (a2r * a1r - a2i * a1i,
            a2r * a1i + a2i * a1r,
            a2r * b1r - a2i * b1i + b2r,
            a2r * b1i + a2i * b1r + b2i)


def s5_bidirectional(u, lam_re, lam_im, log_step, b_re, b_im, c_re, c_im, d_skip, h0_re, h0_im):
    bsz, L = u.shape[:2]
    uf = u.astype(jnp.float32).reshape(bsz, L, N_SSM_GROUPS, SSM_GROUP)
    ys, hr, hi = [], [], []
    for dr in range(2):
        lr = jnp.minimum(lam_re[dr].astype(jnp.float32), -1e-4)
        li = lam_im[dr].astype(jnp.float32)
        step = jnp.exp(log_step[dr].astype(jnp.float32))[:, None]
        mag = jnp.exp(lr * step)
        ab_re = mag * jnp.cos(li * step)
        ab_im = mag * jnp.sin(li * step)
        den = lr * lr + li * li
        nr = ab_re - 1.0
        f_re = (nr * lr + ab_im * li) / den
        f_im = (ab_im * lr - nr * li) / den
        br = b_re[dr].astype(jnp.float32)
        bi = b_im[dr].astype(jnp.float32)
        bb_re = f_re[:, :, None] * br - f_im[:, :, None] * bi
        bb_im = f_re[:, :, None] * bi + f_im[:, :, None] * br
        bu_re = jnp.einsum('blgc,gpc->blgp', uf, bb_re)
        bu_im = jnp.einsum('blgc,gpc->blgp', uf, bb_im)
        reverse = dr == 1
        edge = -1 if reverse else 0
        h0r = h0_re[:, dr].astype(jnp.float32)
        h0i = h0_im[:, dr].astype(jnp.float32)
        bu_re = bu_re.at[:, edge].add(ab_re * h0r - ab_im * h0i)
        bu_im = bu_im.at[:, edge].add(ab_re * h0i + ab_im * h0r)
        a_re = jnp.broadcast_to(ab_re, bu_re.shape)
        a_im = jnp.broadcast_to(ab_im, bu_im.shape)
        _, _, h_re, h_im = lax.associative_scan(
            _complex_affine_combine, (a_re, a_im, bu_re, bu_im), axis=1, reverse=reverse)
        y = (jnp.einsum('blgp,gcp->blgc', h_re, c_re[dr].astype(jnp.float32))
             - jnp.einsum('blgp,gcp->blgc', h_im, c_im[dr].astype(jnp.float32)))
        ys.append(y.reshape(bsz, L, SSM_WIDTH)
                  + d_skip[dr].astype(jnp.float32) * uf.reshape(bsz, L, SSM_WIDTH))
        last = 0 if reverse else -1
        hr.append(h_re[:, last])
        hi.append(h_im[:, last])
    return ys[0] + ys[1], jnp.stack(hr, axis=1), jnp.stack(hi, axis=1)


def layer_forward(x, cond, lp, lam_init, rope, ctx_k, ctx_v, h0_re, h0_im):
    shift1, scale1, gate1, shift2, scale2, gate2 = modulation(cond, lp['w_mod'], lp['b_mod'], x.dtype)
    g = lp['norm_g']
    bsz, L = x.shape[:2]
    h = rmsnorm(x, g[0]) * (1 + scale1) + shift1
    proj = h @ lp['w_in']
    q = proj[..., :ATT_WIDTH].reshape(bsz, L, N_ATT_HEADS, 2, HEAD_DIM)
    k = proj[..., ATT_WIDTH:2 * ATT_WIDTH].reshape(bsz, L, N_ATT_HEADS, 2, HEAD_DIM)
    v = proj[..., 2 * ATT_WIDTH:3 * ATT_WIDTH].reshape(bsz, L, N_ATT_HEADS, 2 * HEAD_DIM)
    u = proj[..., 3 * ATT_WIDTH:]
    keys, vals, q_in = k, v, q
    if rope is not None:
        q_in = apply_rope(q, *rope)
        keys = apply_rope(k, *rope)
    if ctx_k is not None:
        keys = jnp.concatenate([ctx_k.astype(keys.dtype), keys], axis=1)
        vals = jnp.concatenate([ctx_v.astype(vals.dtype), vals], axis=1)
    lp_lam = lp['lam'].astype(jnp.float32)
    lam = (jnp.exp(jnp.sum(lp_lam[0] * lp_lam[1])) - jnp.exp(jnp.sum(lp_lam[2] * lp_lam[3]))
           + lam_init)
    attn = diff_attention(q_in, keys, vals, lam, lp['subln_g'], lam_init)
    y_ssm, hf_re, hf_im = s5_bidirectional(
        u, lp['lam_re'], lp['lam_im'], lp['log_step'], lp['b_re'], lp['b_im'],
        lp['c_re'], lp['c_im'], lp['d_skip'], h0_re, h0_im)
    z = jax.nn.gelu(y_ssm)
    z = z * jax.nn.sigmoid(z @ lp['w_glu'].astype(jnp.float32) + lp['b_glu'].astype(jnp.float32))
    mix = jnp.concatenate([attn, z], axis=-1).astype(x.dtype) @ lp['w_o']
    x = x + gate1 * rmsnorm(mix, g[1])
    h = rmsnorm(x, g[2]) * (1 + scale2) + shift2
    gt, up = jnp.split(h @ lp['w_ffn_in'], 2, axis=-1)
    f = (jax.nn.silu(gt) * up) @ lp['w_ffn_out']
    x = x + gate2 * rmsnorm(f, g[3])
    return x, k, v, hf_re, hf_im


def setup_inputs(seed: int = 0) -> dict:
    key = jax.random.key(seed)
    ks = jax.random.split(key, 32)
    f32 = jnp.float32

    def nrm(k, shape, scale):
        return scale * jax.random.normal(k, shape, f32)

    G, P = N_SSM_GROUPS, SSM_STATE
    n = jnp.arange(P, dtype=f32)
    return {
        'x_prompt': nrm(ks[0], (BATCH, SEQ, D_MODEL), 1.0),
        'x_sample': nrm(ks[1], (DEC_BATCH, DEC_SEQ, D_MODEL), 1.0),
        'cache_k': nrm(ks[2], (DEC_BATCH, DEPTH, PAST_LEN, 2 * N_ATT_HEADS, HEAD_DIM), 1.0),
        'cache_v': nrm(ks[3], (DEC_BATCH, DEPTH, PAST_LEN, N_ATT_HEADS, 2 * HEAD_DIM), 1.0),
        'state_ssm_re': nrm(ks[4], (DEC_BATCH, DEPTH, 2, G, P), 0.1),
        'state_ssm_im': nrm(ks[5], (DEC_BATCH, DEPTH, 2, G, P), 0.1),
        'c': nrm(ks[6], (DEC_BATCH, D_MODEL), 1.0),
        'c_ctx': nrm(ks[7], (D_MODEL,), 1.0),
        'w_mod': nrm(ks[8], (DEPTH, D_MODEL, N_MOD * D_MODEL), 0.5 * D_MODEL ** -0.5),
        'b_mod': nrm(ks[9], (DEPTH, N_MOD * D_MODEL), 0.01),
        'norm_g': 1.0 + nrm(ks[10], (DEPTH, 4, D_MODEL), 0.01),
        'w_in': nrm(ks[11], (DEPTH, D_MODEL, IN_WIDTH), D_MODEL ** -0.5),
        'lam_params': nrm(ks[12], (DEPTH, 4, HEAD_DIM), 0.1),
        'subln_g': 1.0 + nrm(ks[13], (DEPTH, 2 * HEAD_DIM), 0.01),
        'ssm_lambda_re': -0.5 + nrm(ks[14], (DEPTH, 2, G, P), 0.01),
        'ssm_lambda_im': math.pi * n + nrm(ks[15], (DEPTH, 2, G, P), 0.01),
        'ssm_log_step': jax.random.uniform(ks[16], (DEPTH, 2, G), f32,
                                           math.log(1e-3), math.log(1e-1)),
        'ssm_b_re': nrm(ks[17], (DEPTH, 2, G, P, SSM_GROUP), (2 * SSM_GROUP) ** -0.5),
        'ssm_b_im': nrm(ks[18], (DEPTH, 2, G, P, SSM_GROUP), (2 * SSM_GROUP) ** -0.5),
        'ssm_c_re': nrm(ks[19], (DEPTH, 2, G, SSM_GROUP, P), (2 * P) ** -0.5),
        'ssm_c_im': nrm(ks[20], (DEPTH, 2, G, SSM_GROUP, P), (2 * P) ** -0.5),
        'ssm_d': nrm(ks[21], (DEPTH, 2, SSM_WIDTH), 0.5),
        'w_glu': nrm(ks[22], (DEPTH, SSM_WIDTH, SSM_WIDTH), SSM_WIDTH ** -0.5),
        'b_glu': nrm(ks[23], (DEPTH, SSM_WIDTH), 0.01),
        'w_o': nrm(ks[24], (DEPTH, MIX_WIDTH, D_MODEL), MIX_WIDTH ** -0.5),
        'w_ffn_in': nrm(ks[25], (DEPTH, D_MODEL, 2 * D_FF), D_MODEL ** -0.5),
        'w_ffn_out': nrm(ks[26], (DEPTH, D_FF, D_MODEL), D_FF ** -0.5),
    }


def reference(x_prompt, x_sample, cache_k, cache_v, state_ssm_re, state_ssm_im, c, c_ctx,
              w_mod, b_mod, norm_g, w_in, lam_params, subln_g,
              ssm_lambda_re, ssm_lambda_im, ssm_log_step, ssm_b_re, ssm_b_im,
              ssm_c_re, ssm_c_im, ssm_d, w_glu, b_glu, w_o, w_ffn_in, w_ffn_out):
    n_rows = x_sample.shape[1] // GRID_W
    rope = axial_rope_tables(n_rows)
    bp, lp_len = x_prompt.shape[:2]
    bd, past = cache_k.shape[0], cache_k.shape[2]
    zeros_h = jnp.zeros((bp, 2, N_SSM_GROUPS, SSM_STATE), jnp.float32)
    xp, xs = x_prompt, x_sample
    ks_out, vs_out, hr_out, hi_out = [], [], [], []
    for l in range(DEPTH):
        lam_init = 0.8 - 0.6 * math.exp(-0.3 * l)
        lp = {
            'w_mod': w_mod[l], 'b_mod': b_mod[l], 'norm_g': norm_g[l], 'w_in': w_in[l],
            'lam': lam_params[l], 'subln_g': subln_g[l],
            'lam_re': ssm_lambda_re[l], 'lam_im': ssm_lambda_im[l], 'log_step': ssm_log_step[l],
            'b_re': ssm_b_re[l], 'b_im': ssm_b_im[l], 'c_re': ssm_c_re[l], 'c_im': ssm_c_im[l],
            'd_skip': ssm_d[l], 'w_glu': w_glu[l], 'b_glu': b_glu[l], 'w_o': w_o[l],
            'w_ffn_in': w_ffn_in[l], 'w_ffn_out': w_ffn_out[l],
        }
        xp, k_ctx, v_ctx, h_re, h_im = layer_forward(
            xp, c_ctx[None, :], lp, lam_init, None, None, None, zeros_h, zeros_h)
        ks_out.append(k_ctx.reshape(bp, lp_len, 2 * N_ATT_HEADS, HEAD_DIM))
        vs_out.append(v_ctx)
        hr_out.append(h_re)
        hi_out.append(h_im)
        ck = cache_k[:, l].reshape(bd, past, N_ATT_HEADS, 2, HEAD_DIM)
        cv = cache_v[:, l]
        xs, _, _, _, _ = layer_forward(
            xs, c, lp, lam_init, rope, ck, cv, state_ssm_re[:, l], state_ssm_im[:, l])
    new_cache_k = jnp.stack(ks_out, axis=1)
    new_cache_v = jnp.stack(vs_out, axis=1)
    new_state_ssm_re = jnp.stack(hr_out, axis=1)
    new_state_ssm_im = jnp.stack(hi_out, axis=1)
    return (xp, xs, new_cache_k, new_cache_v, new_state_ssm_re, new_state_ssm_im)
```

```python
import math
import numpy as np
import concourse.bass as bass
import concourse.mybir as mybir
from concourse.bass_utils import run_bass_kernel_spmd

F32 = mybir.dt.float32
BF16 = mybir.dt.bfloat16
AF = mybir.ActivationFunctionType
ALU = mybir.AluOpType
AX = mybir.AxisListType

NS_DMA = 40
ENGS = ['pe', 'act', 'dve', 'pool', 'sp']
SAME_SYNC = True
SAME_DIST = 10 ** 9
T1 = 16
VW = 130
MAGIC = 12582912.0
TWO_PI = 2.0 * math.pi


class Buf:
    __slots__ = ('name', 'w', 'r', 'lo', 'hi', 'excl')

    def __init__(self, name, lo=0, hi=0, excl=False):
        self.name = name
        self.excl = excl
        self.w = None
        self.r = {}
        self.lo = lo
        self.hi = hi


class Sched:
    def __init__(self):
        self.ops = {e: [] for e in ENGS}
        self.ndma = 0
        self.out_dma = []

    def add(self, eng, emit, reads=(), writes=(), dma=False, is_out=False, multi=False):
        ex = [b for b in reads if b.excl]
        if ex:
            reads = [b for b in reads if not b.excl]
            writes = list(writes) + [b for b in ex if b not in writes]
        idx = len(self.ops[eng])
        deps = set()
        for b in reads:
            if b.w is not None:
                deps.add(b.w)
        for b in writes:
            if b.w is not None:
                deps.add(b.w)
            deps.update(b.r.values())
        if dma:
            k = self.ndma
            self.ndma += 1
            key = ('dma', k)
            if k >= NS_DMA:
                deps.add(('dma', k - NS_DMA))
            if is_out:
                self.out_dma.append(key)
        else:
            k = None
            key = (eng, idx)
        self.ops[eng].append(dict(emit=emit, deps=deps, dma=k, inc=False, cnt=0, multi=multi))
        for b in reads:
            if dma:
                b.r[key] = key
            else:
                b.r[eng] = key
        for b in writes:
            b.w = key
            b.r = {}
        return key

    def finalize(self, nc, sems, dsems, engines):
        for eng in ENGS:
            for op_idx, op in enumerate(self.ops[eng]):
                nd = set()
                for d in op['deps']:
                    if d[0] == 'dma':
                        nd.add(d)
                        continue
                    if d[0] == eng and (eng == 'pe' or not SAME_SYNC) and op['dma'] is None:
                        continue
                    if d[0] == eng and op['dma'] is None and (op_idx - d[1]) >= SAME_DIST:
                        continue
                    self.ops[d[0]][d[1]]['inc'] = True
                    nd.add(d)
                op['deps'] = nd
        for eng in ENGS:
            c = 0
            for op in self.ops[eng]:
                if op['inc']:
                    c += 1
                op['cnt'] = c

        def run(eng):
            def body(e):
                known = {}
                for op in self.ops[eng]:
                    need = {}
                    for d in op['deps']:
                        if d[0] == 'dma':
                            slot = d[1] % NS_DMA
                            val = 16 * (d[1] // NS_DMA + 1)
                            sk = ('d', slot)
                        else:
                            val = self.ops[d[0]][d[1]]['cnt']
                            sk = ('e', d[0])
                        if need.get(sk, 0) < val:
                            need[sk] = val
                    todo = []
                    for sk, val in need.items():
                        if known.get(sk, 0) >= val:
                            continue
                        known[sk] = val
                        todo.append((dsems[sk[1]] if sk[0] == 'd' else sems[sk[1]], val))
                    if op['multi']:
                        for (s, val) in todo:
                            e.wait_ge(s, val)
                        todo = []
                    for (s, val) in todo[:-1]:
                        e.wait_ge(s, val)
                    ins = op['emit'](e)
                    if todo:
                        ins._wait_ge(todo[-1][0], todo[-1][1])
                    if op['inc']:
                        ins.then_inc(sems[eng], 1)
                    if op['dma'] is not None:
                        ins.then_inc(dsems[op['dma'] % NS_DMA], 16)
                if eng == 'sp':
                    fin = {}
                    for d in self.out_dma:
                        slot = d[1] % NS_DMA
                        val = 16 * (d[1] // NS_DMA + 1)
                        fin[slot] = max(fin.get(slot, 0), val)
                    for slot, val in fin.items():
                        e.wait_ge(dsems[slot], val)
            return body
        with nc.Block() as block:
            block.tensor(run('pe'))
            block.scalar(run('act'))
            block.vector(run('dve'))
            block.gpsimd(run('pool'))
            block.sync(run('sp'))


class Arena:
    def __init__(self, ap2d, nwords):
        self.base = ap2d
        self.n = nwords
        self.top = 0
        self.ttop = nwords
        self.retired = []
        self.live = []

    def alloc(self, name, nelem, dtype=F32, top=False):
        nb = nelem * (2 if dtype == BF16 else 4)
        nw = (nb + 31) // 32 * 8
        if top:
            hi = self.ttop
            lo = hi - nw
            assert lo >= self.top, f"arena overflow (top) at {name}"
            self.ttop = lo
        else:
            lo = self.top
            hi = lo + nw
            assert hi <= self.ttop, f"arena overflow at {name}: {hi} > {self.ttop}"
            self.top = hi
        v = self.base[:, lo:hi]
        if dtype == BF16:
            v = v.bitcast(BF16)
        v = v[:, 0:nelem]
        b = Buf(name, lo, hi)
        self.live.append(b)
        for (rlo, rhi, keys) in self.retired:
            if rlo < hi and lo < rhi:
                for i, k in enumerate(keys):
                    b.r[('ret', rlo, i)] = k
        return v, b

    def mark(self):
        return self.top

    def release(self, mark, bufs=None, top=False):
        keep = []
        for b in self.live:
            dead = (b.hi <= mark and b.lo >= self.ttop) if top else (b.lo >= mark and b.hi <= self.top)
            if top:
                dead = b.lo >= self.ttop and b.hi <= mark and b.lo >= self.top
            if dead:
                keys = list(b.r.values())
                if b.w is not None:
                    keys.append(b.w)
                if keys:
                    self.retired.append((b.lo, b.hi, keys))
            else:
                keep.append(b)
        self.live = keep
        if top:
            self.ttop = mark
        else:
            self.top = mark


def V(t2d, off, dims):
    return bass.AP(tensor=t2d.tensor, offset=t2d.offset + off,
                   ap=[list(t2d.ap[0])] + [[int(s), int(n)] for (s, n) in dims])


D = 1024
NT_OWN = 8
NTOK_ALL = 2560


def build_program(stop_after=99, debug=(), bisect={}):
    nc = bass.Bass("TRN2", target_bir_lowering=False)
    S = Sched()

    def din(name, shape):
        return nc.dram_tensor(name, list(shape), F32, kind="ExternalInput").ap()

    def dout(name, shape):
        return nc.dram_tensor(name, list(shape), F32, kind="ExternalOutput").ap()

    xp = din("xp", [512, D])
    xs = din("xs", [2048, D])
    ropec = din("ropec", [128, 2048])
    ropes = din("ropes", [128, 2048])
    ck = din("ck", [256, 512])
    cv = din("cv", [256, 512])
    h0re = din("h0re", [32, 128])
    h0im = din("h0im", [32, 128])
    cond2 = din("cond2", [2, D])
    w_mod = din("w_mod", [D, 6 * D])
    b_mod = din("b_mod", [6 * D])
    norm_g = din("norm_g", [4, D])
    w_in = din("w_in", [D, 2048])
    lam_params = din("lam_params", [4, 64])
    subln_g = din("subln_g", [128])
    s_lre = din("s_lre", [32, 128])
    s_lim = din("s_lim", [32, 128])
    s_lstep = din("s_lstep", [32, 2])
    s_bre = din("s_bre", [2, 32, 64, 16])
    s_bim = din("s_bim", [2, 32, 64, 16])
    s_cre = din("s_cre", [2, 32, 16, 64])
    s_cim = din("s_cim", [2, 32, 16, 64])
    s_d = din("s_d", [2, 512])
    w_glu = din("w_glu", [512, 512])
    b_glu = din("b_glu", [512])
    w_o = din("w_o", [D, D])
    w_ffn_in = din("w_ffn_in", [D, 5632])
    w_ffn_out = din("w_ffn_out", [2816, D])
    c_ident = din("c_ident", [128, 128])
    c_rrot = din("c_rrot", [128, 128])
    c_tab = din("c_tab", [128, 64])
    c_flags = din("c_flags", [128, 32])

    yp = dout("yp", [512, D])
    ys = dout("ys", [512, D])
    o_nk = dout("o_nk", [512, 512])
    o_nv = dout("o_nv", [512, 512])
    o_sre = dout("o_sre", [2, 32, 128])
    o_sim = dout("o_sim", [2, 32, 128])
    dbg = {}
    for (nm, shp) in debug:
        dt_ = BF16 if nm.startswith('b_') else F32
        dbg[nm[2:] if nm.startswith('b_') else nm] = nc.dram_tensor(nm, list(shp), dt_, kind="ExternalOutput").ap()

    NW = 52400
    import contextlib
    with contextlib.ExitStack() as es:
        arena_t = es.enter_context(nc.sbuf_tensor("arena", [128, NW], F32))
        psb = [es.enter_context(nc.psum_tensor(f"psb{i}", [128, 512], F32)) for i in range(8)]
        sems = {e: es.enter_context(nc.semaphore(f"s_{e}")) for e in ENGS}
        dsems = [es.enter_context(nc.semaphore(f"d_{i}")) for i in range(NS_DMA)]
        A = Arena(arena_t[:, :], NW)
        PS = [psb[i][:, :] for i in range(8)]
        PSB = [Buf(f"ps{i}", excl=True) for i in range(8)]
        PSH = [PS[i].bitcast(BF16) for i in range(8)]

        def dma(q, out, in_, reads, writes, is_out=False, nc_ok=False):
            def em(e, out=out, in_=in_, nc_ok=nc_ok):
                kw = {}
                if nc_ok:
                    kw['allow_slow_non_contiguous'] = True
                return e.dma_start(out=out, in_=in_, **kw)
            assert q in ('sp', 'act')
            return S.add(q, em, reads, writes, dma=True, is_out=is_out)

        wst = {'n': 0, 'bufs': None}

        def load_cast(dst, dst_b, src, ncols):
            for c0 in range(0, ncols, 1024):
                n = min(1024, ncols - c0)
                st_t, st_b = wst['bufs'][wst['n'] % len(wst['bufs'])]
                wst['n'] += 1
                dma('sp', st_t[:, 0:n], src[:, c0:c0 + n], (), (st_b,))
                cp('pool', dst[:, c0:c0 + n], st_t[:, 0:n], (st_b,), (dst_b,))

        def mm(out, lhsT, rhs, start, stop, reads, writes):
            return S.add('pe', lambda e, o=out, l=lhsT, r=rhs, s=start, t=stop:
                         e.matmul(o, lhsT=l, rhs=r, start=s, stop=t, skip_group_check=True), reads, writes)

        def tp(out, in_, ident, reads, writes):
            return S.add('pe', lambda e, o=out, i=in_, d=ident: e.transpose(o, i, d), reads, writes)

        def act(out, in_, func, reads, writes, bias=None, scale=None, accum_out=None, eng='act'):
            def em(e, out=out, in_=in_, func=func, bias=bias, scale=scale, accum_out=accum_out):
                kw = {}
                if bias is not None:
                    kw['bias'] = bias
                if scale is not None:
                    kw['scale'] = scale
                if accum_out is not None:
                    kw['accum_out'] = accum_out
                return e.activation(out=out, in_=in_, func=func, **kw)
            return S.add(eng, em, reads, writes, multi=(accum_out is not None))

        def ts(eng, out, in0, s1, s2, op0, op1, reads, writes):
            def em(e, out=out, in0=in0, s1=s1, s2=s2, op0=op0, op1=op1):
                if op1 is None:
                    return e.tensor_scalar(out, in0, s1, None, op0)
                return e.tensor_scalar(out, in0, s1, s2, op0, op1)
            return S.add(eng, em, reads, writes)

        def tt(eng, out, in0, in1, op, reads, writes):
            return S.add(eng, lambda e, o=out, a=in0, b=in1, p=op: e.tensor_tensor(o, a, b, p), reads, writes)

        def stt(eng, out, in0, scalar, in1, op0, op1, reads, writes):
            return S.add(eng, lambda e, o=out, a=in0, s=scalar, b=in1, p0=op0, p1=op1:
                         e.scalar_tensor_tensor(o, a, s, b, p0, p1), reads, writes)

        def cp(eng, out, in_, reads, writes):
            if eng == 'act':
                return S.add(eng, lambda e, o=out, i=in_: e.activation(out=o, in_=i, func=AF.Copy), reads, writes)
            return S.add(eng, lambda e, o=out, i=in_: e.tensor_copy(o, i), reads, writes)

        def memset(eng, ap, val, writes):
            return S.add(eng, lambda e, a=ap, v=val: e.memset(a, v), (), writes)

        def recip(out, in_, reads, writes):
            return S.add('dve', lambda e, o=out, i=in_: e.reciprocal(o, i), reads, writes)

        def scan(out, d0, d1, reads, writes):
            return S.add('dve', lambda e, o=out, a=d0, b=d1:
                         e.tensor_tensor_scan(out=o, data0=a, data1=b, initial=0.0, op0=ALU.mult, op1=ALU.add),
                         reads, writes)

        rr = {'n': 0}

        def ev_eng():
            rr['n'] += 1
            return 'act' if rr['n'] % 2 else 'dve'

        psrot = {'t': 0, 'p': 0}

        def next_tps():
            psrot['t'] += 1
            return psrot['t'] % 2

        def next_ps():
            psrot['p'] += 1
            if psrot.get('hold7'):
                return 2 + psrot['p'] % 5
            return 2 + psrot['p'] % 6

        ident, b_ident = A.alloc("ident", 128)
        identb, b_identb = A.alloc("identb", 128, BF16)
        rrot, b_rrot = A.alloc("rrot", 128, BF16)
        ctab, b_ctab = A.alloc("ctab", 64)
        flags, b_flags = A.alloc("flags", 32)
        epsv, b_eps = A.alloc("epsv", 1)
        wst['bufs'] = [A.alloc(f"wst{i}", 1024) for i in range(2)]
        awreg, _b_aw = A.alloc("awreg", 8192)
        A.live.remove(_b_aw)
        AW = Arena(awreg, 8192)
        dma('sp', ident, c_ident[:, :], (), (b_ident,))
        load_cast(identb, b_identb, c_ident, 128)
        load_cast(rrot, b_rrot, c_rrot, 128)
        dma('sp', ctab, c_tab[:, :], (), (b_ctab,))
        dma('sp', flags, c_flags[:, :], (), (b_flags,))
        memset('dve', epsv, 1e-6, (b_eps,))

        winsb = [AW.alloc(f"win{k}", 2048, BF16) for k in range(8)]
        for k in range(8):
            load_cast(winsb[k][0], winsb[k][1], w_in[k * 128:(k + 1) * 128, :], 2048)
        condT, b_condT = A.alloc("condT", 16)
        scondT, b_scondT = A.alloc("scondT", 16, BF16)
        bmodT, b_bmodT = A.alloc("bmodT", 48)
        ngT, b_ngT = A.alloc("ngT", 32)
        modT, b_modT = A.alloc("modT", 96)
        sc1, b_sc1 = A.alloc("sc1", 16)
        sc2, b_sc2 = A.alloc("sc2", 16)
        for c in range(2):
            dma('sp', V(condT, c, [(2, 8)]), cond2[c, :].rearrange("(k p) -> p k", p=128), (), (b_condT,), nc_ok=True)
        dma('sp', bmodT, b_mod.rearrange("(t p) -> p t", p=128), (), (b_bmodT,), nc_ok=True)
        for g in range(4):
            dma('sp', ngT[:, g * 8:(g + 1) * 8], norm_g[g, :].rearrange("(k p) -> p k", p=128), (), (b_ngT,), nc_ok=True)
        act(scondT, condT, AF.Silu, (b_condT,), (b_scondT,))
        scond32, b_scond32 = A.alloc("scond32", 16)
        mark0 = A.mark()
        act(scond32, condT, AF.Silu, (b_condT,), (b_scond32,))
        wm = [A.alloc(f"wm{i}", 2048) for i in range(3)]
        for kc in range(8):
            wt, wb = wm[kc % 3]
            dma('act', wt, w_mod[kc * 128:(kc + 1) * 128, 0:2048], (), (wb,))
            for ct in range(16):
                mm(V(PS[0], ct * 2, [(1, 2)]), wt[:, ct * 128:(ct + 1) * 128], V(scond32, kc * 2, [(1, 2)]),
                   (kc == 0 and ct == 0), (kc == 7), (wb, b_scond32), (PSB[0],))
        tt('dve', V(modT, 0, [(2, 16), (1, 2)]), V(PS[0], 0, [(2, 16), (1, 2)]), V(bmodT, 0, [(1, 16), (0, 2)]),
           ALU.add, (PSB[0], b_bmodT), (b_modT,))

        def mk_sc(sc, bsc, comp, gi):
            ts('dve', sc, V(modT, comp * 16, [(1, 16)]), 1.0, None, ALU.add, None, (b_modT,), (bsc,))
            tt('dve', V(sc, 0, [(2, 8), (1, 2)]), V(sc, 0, [(2, 8), (1, 2)]), V(ngT, gi * 8, [(1, 8), (0, 2)]),
               ALU.mult, (bsc, b_ngT), (bsc,))
        mk_sc(sc1, b_sc1, 1, 0)
        A.release(mark0)


        if stop_after == 10:
            S.finalize(nc, sems, dsems, None)
            return nc
        zT, b_zT = A.alloc("zT", 4 * 1024, BF16, top=True)
        attnO, b_attnO = A.alloc("attnO", 8 * 512, BF16, top=True)
        markB = A.mark()
        uT, b_uT = A.alloc("uT", 4 * NTOK_ALL, BF16)
        uTA, b_uTA = A.alloc("uTA", 4 * 1024, BF16)
        markC = A.mark()
        qT, b_qT = A.alloc("qT", 4 * 1024, BF16)
        kTp, b_kTp = A.alloc("kTp", 4 * 512, BF16)
        kTs, b_kTs = A.alloc("kTs", 4 * 2304, BF16)
        Vp, b_Vp = A.alloc("Vp", 4 * 4 * VW, BF16)
        Vs, b_Vs = A.alloc("Vs", 18 * 4 * VW, BF16)
        wmB = [A.alloc(f"wmB{i}", 2048) for i in range(2)]
        psrot['hold7'] = True

        def modB_dma(p):
            kc, c3 = p // 2, 1 + p % 2
            wt, wb = wmB[p % 2]
            dma('sp', wt, w_mod[kc * 128:(kc + 1) * 128, c3 * 2048:(c3 + 1) * 2048], (), (wb,))

        def modB_mm(p):
            kc, c3 = p // 2, 1 + p % 2
            wt, wb = wmB[p % 2]
            for ct in range(c3 * 16, (c3 + 1) * 16):
                mm(V(PS[7], (ct - 16) * 2, [(1, 2)]), wt[:, (ct - c3 * 16) * 128:(ct - c3 * 16 + 1) * 128], V(scond32, kc * 2, [(1, 2)]),
                   (p == 0 and ct == 16), (kc == 7), (wb, b_scond32), (PSB[7],))

        mark1 = A.mark()
        hTx = [A.alloc(f"hTx{i}", 8 * 512, BF16) for i in range(2)]
        xst = [A.alloc(f"xst{i}", D) for i in range(2)]
        xnb = [A.alloc(f"xnb{i}", D, BF16) for i in range(2)]
        ropet = [A.alloc(f"ropet{i}", 512) for i in range(2)]
        tmpf = [A.alloc(f"tmpf{i}", 512) for i in range(4)]
        rawb = [A.alloc(f"rawb{i}", 512, BF16) for i in range(2)]
        stg = [A.alloc(f"stg{i}", 512) for i in range(2)]
        small = [A.alloc(f"small{i}", 4) for i in range(4)]
        ckb, b_ckb = A.alloc("ckb", 2 * 512, BF16)

        if dbg:
            for (t_, b_) in [(qT, b_qT), (kTp, b_kTp), (kTs, b_kTs), (Vp, b_Vp), (Vs, b_Vs), (uT, b_uT), (uTA, b_uTA)]:
                memset('pool', t_, 0.0, (b_,))
        memset('dve', V(Vp, 128, [(VW, 16)]), 1.0, (b_Vp,))
        memset('dve', V(Vs, 128, [(VW, 72)]), 1.0, (b_Vs,))
        for t in range(2):
            load_cast(ckb[:, t * 512:(t + 1) * 512], b_ckb, ck[t * 128:(t + 1) * 128, :], 512)
            st_t, st_b = wst['bufs'][wst['n'] % 2]
            wst['n'] += 1
            dma('sp', st_t[:, 0:512], cv[t * 128:(t + 1) * 128, :], (), (st_b,))
            cp('pool', V(Vs, t * 4 * VW, [(VW, 4), (1, 128)]), V(st_t, 0, [(128, 4), (1, 128)]), (st_b,), (b_Vs,))
        for t in range(2):
            for h in range(4):
                tp(PSH[0][:, (t * 4 + h) * 128:(t * 4 + h + 1) * 128], ckb[:, t * 512 + h * 128:t * 512 + (h + 1) * 128],
                   identb, (b_ckb, b_identb), (PSB[0],))
        for t in range(2):
            cp('act', V(kTs, t * 128, [(2304, 4), (1, 128)]), V(PSH[0], t * 512, [(128, 4), (1, 128)]), (PSB[0],), (b_kTs,))

        def norm_tile(src_rows, cidx, hdst, hb, tok0, ntok_total, ti):
            xs_t, xs_b = xst[ti % 2]
            xn_t, xn_b = xnb[ti % 2]
            sm, sm_b = small[ti % 4]
            dma('sp', xs_t, src_rows, (), (xs_b,))
            act(xn_t, xs_t, AF.Square, (xs_b,), (xn_b, sm_b), accum_out=sm[:, 0:1])
            act(sm[:, 1:2], sm[:, 0:1], AF.Sqrt, (sm_b, b_eps), (sm_b,), bias=epsv[:, 0:1], scale=1.0 / D)
            recip(sm[:, 2:3], sm[:, 1:2], (sm_b,), (sm_b,))
            ts('dve', xn_t, xs_t, sm[:, 2:3], None, ALU.mult, None, (xs_b, sm_b), (xn_b,))
            pb = next_tps()
            for kc in range(8):
                tp(PSH[pb][:, kc * 128:(kc + 1) * 128], xn_t[:, kc * 128:(kc + 1) * 128], identb, (xn_b, b_identb), (PSB[pb],))
            for kc in range(8):
                ts('dve', hdst[:, kc * ntok_total + tok0:kc * ntok_total + tok0 + 128], PSH[pb][:, kc * 128:(kc + 1) * 128],
                   sc1[:, kc * 2 + cidx:kc * 2 + cidx + 1], modT[:, kc * 2 + cidx:kc * 2 + cidx + 1], ALU.mult, ALU.add,
                   (PSB[pb], b_sc1, b_modT), (hb,))

        def proj_ftile(f, hsrc, hb, ntok_total, tok0):
            pb = next_ps()
            for kc in range(8):
                mm(PS[pb], winsb[kc][0][:, f * 128:(f + 1) * 128], hsrc[:, kc * ntok_total + tok0:kc * ntok_total + tok0 + 512],
                   kc == 0, kc == 7, (winsb[kc][1], hb), (PSB[pb],))
            return pb

        ropecnt = {'n': 0}

        def rope_evac(pb, dst, dst_b, ct, st, cb, sb):
            i = ropecnt['n']
            ropecnt['n'] += 1
            rb, rb_b = rawb[i % 2]
            t1, t1_b = tmpf[(2 * i) % 4]
            t2, t2_b = tmpf[(2 * i + 1) % 4]
            cp('act', rb, PS[pb], (PSB[pb],), (rb_b,))
            pb2 = next_ps()
            mm(PS[pb2], rrot, rb, True, True, (b_rrot, rb_b), (PSB[pb2],))
            tt('dve', t1, PS[pb], ct, ALU.mult, (PSB[pb], cb), (t1_b,))
            tt('dve', t2, PS[pb2], st, ALU.mult, (PSB[pb2], sb), (t2_b,))
            tt('pool', dst, t1, t2, ALU.add, (t1_b, t2_b), (dst_b,))

        blocks = [('P', 0), ('S', 1), ('O', 2), ('O', 3), ('O', 4)]
        if stop_after <= 1:
            blocks = blocks[:bisect.get('nblk', 5)]
        tcount = {'n': 0}

        def blk_vars(kind, bi):
            own = kind != 'O'
            cidx = 0 if kind == 'P' else 1
            hsrc, hb = hTx[bi % 2]
            return own, cidx, hsrc, hb, 512, 0, (0 if kind == 'P' else 512)

        def norm_block(kind, bi):
            own, cidx, hsrc, hb, ntt, t0, q0 = blk_vars(kind, bi)
            ti = tcount['n']
            for t in range(4):
                if kind == 'P':
                    rows = xp[t * 128:(t + 1) * 128, :]
                else:
                    r0 = (bi - 1) * 512 + t * 128
                    rows = xs[r0:r0 + 128, :]
                norm_tile(rows, cidx, hsrc, hb, t0 + t * 128, ntt, ti)
                if 1 <= ti <= 16:
                    modB_mm(ti - 1)
                if ti < 16:
                    modB_dma(ti)
                ti += 1
                tcount['n'] = ti

        def proj_block(kind, bi):
            own, cidx, hsrc, hb, ntt, t0, q0 = blk_vars(kind, bi)
            if bisect.get('noproj'):
                return
            utok0 = bi * 512
            if kind != 'P':
                s0 = (bi - 1) * 512
                ct, cb = ropet[0]
                st, sb = ropet[1]
                dma('sp', ct, ropec[:, s0:s0 + 512], (), (cb,))
                dma('sp', st, ropes[:, s0:s0 + 512], (), (sb,))
            if own and not bisect.get('noqku'):
                for h in range(4):
                    pb = proj_ftile(h, hsrc, hb, ntt, t0)
                    dst = qT[:, h * 1024 + q0:h * 1024 + q0 + 512]
                    if kind == 'P':
                        cp(ev_eng(), dst, PS[pb], (PSB[pb],), (b_qT,))
                    else:
                        rope_evac(pb, dst, b_qT, ct, st, cb, sb)
            for h in range(0 if bisect.get('noqku') else 4):
                pb = proj_ftile(4 + h, hsrc, hb, ntt, t0)
                if kind == 'P':
                    cp(ev_eng(), kTp[:, h * 512:(h + 1) * 512], PS[pb], (PSB[pb],), (b_kTp,))
                else:
                    k0 = 256 + (bi - 1) * 512
                    rope_evac(pb, kTs[:, h * 2304 + k0:h * 2304 + k0 + 512], b_kTs, ct, st, cb, sb)
            for jj in range(0 if bisect.get('noqku') else 4):
                pb = proj_ftile(12 + jj, hsrc, hb, ntt, t0)
                ee = ev_eng()
                cp(ee, V(uT, jj * NTOK_ALL + bi * 32, [(1, 32), (160, 16)]), V(PS[pb], 0, [(16, 32), (1, 16)]), (PSB[pb],), (b_uT,))
                if own:
                    cp(ee, V(uTA, jj * 1024 + bi * 512, [(1, 32), (32, 16)]), V(PS[pb], 0, [(16, 32), (1, 16)]), (PSB[pb],), (b_uTA,))
            for t in range(0 if bisect.get('nov') else 4):
                tok = t0 + t * 128
                pb = next_ps()
                for kc in range(8):
                    mm(PS[pb], hsrc[:, kc * ntt + tok:kc * ntt + tok + 128], winsb[kc][0][:, 1024:1536], kc == 0, kc == 7,
                       (winsb[kc][1], hb), (PSB[pb],))
                if kind == 'P':
                    vdst = V(Vp, t * 4 * VW, [(VW, 4), (1, 128)])
                    vb = b_Vp
                else:
                    vdst = V(Vs, (2 + (bi - 1) * 4 + t) * 4 * VW, [(VW, 4), (1, 128)])
                    vb = b_Vs
                cp('act', vdst, V(PS[pb], 0, [(128, 4), (1, 128)]), (PSB[pb],), (vb,))
                if kind == 'P':
                    sg, sg_b = stg[0]
                    cp('dve', sg, PS[pb], (PSB[pb],), (sg_b,))
                    dma('sp', o_nv[t * 128:(t + 1) * 128, :], sg, (sg_b,), (), is_out=True)
                    pb = next_ps()
                    for kc in range(8):
                        mm(PS[pb], hsrc[:, kc * ntt + tok:kc * ntt + tok + 128], winsb[kc][0][:, 512:1024], kc == 0, kc == 7,
                           (winsb[kc][1], hb), (PSB[pb],))
                    sg, sg_b = stg[1]
                    cp('dve', sg, PS[pb], (PSB[pb],), (sg_b,))
                    dma('sp', o_nk[t * 128:(t + 1) * 128, :], sg, (sg_b,), (), is_out=True)


        norm_block(*blocks[0])
        for ib, (kind, bi) in enumerate(blocks):
            if ib + 1 < len(blocks):
                norm_block(*blocks[ib + 1])
            proj_block(kind, bi)
        if 'qT' in dbg:
            dma('sp', dbg['qT'], qT, (b_qT,), (), is_out=True)
            dma('sp', dbg['kTs'], kTs, (b_kTs,), (), is_out=True)
            dma('sp', dbg['kTp'], kTp, (b_kTp,), (), is_out=True)
            dma('sp', dbg['Vs'], Vs, (b_Vs,), (), is_out=True)
            dma('sp', dbg['Vp'], Vp, (b_Vp,), (), is_out=True)
            dma('sp', dbg['uT'], uT, (b_uT,), (), is_out=True)
        tt('dve', V(modT, 32, [(2, 32), (1, 2)]), V(PS[7], 0, [(2, 32), (1, 2)]), V(bmodT, 16, [(1, 32), (0, 2)]),
           ALU.add, (PSB[7], b_bmodT), (b_modT,))
        mk_sc(sc2, b_sc2, 4, 2)
        psrot['hold7'] = False
        A.release(mark1)
        AW.release(0)


        if stop_after == 12:
            S.finalize(nc, sems, dsems, None)
            return nc
        lamt, b_lamt = A.alloc("lamt", 256 + 16)
        gsub, b_gsub = A.alloc("gsub", 128)
        mark3 = A.mark()
        PTb = [A.alloc(f"PT{i}", 512, BF16) for i in range(8)]
        ctmp = [A.alloc(f"ctmp{i}", 128 + 128 + 8) for i in range(2)]
        dma('sp', lamt[:, 0:256], lam_params.rearrange("a b -> (a b)").partition_broadcast(128), (), (b_lamt,))
        dma('sp', gsub, subln_g.partition_broadcast(128), (), (b_gsub,))
        ts('dve', gsub, gsub, 0.8, None, ALU.mult, None, (b_gsub,), (b_gsub,))
        tt('dve', lamt[:, 0:64], lamt[:, 0:64], lamt[:, 64:128], ALU.mult, (b_lamt,), (b_lamt,))
        tt('dve', lamt[:, 128:192], lamt[:, 128:192], lamt[:, 192:256], ALU.mult, (b_lamt,), (b_lamt,))
        S.add('dve', lambda e: e.reduce_sum(lamt[:, 256:257], lamt[:, 0:64], AX.X), (b_lamt,), (b_lamt,))
        S.add('dve', lambda e: e.reduce_sum(lamt[:, 257:258], lamt[:, 128:192], AX.X), (b_lamt,), (b_lamt,))
        act(lamt[:, 258:260], lamt[:, 256:258], AF.Exp, (b_lamt,), (b_lamt,))
        stt('dve', lamt[:, 260:261], lamt[:, 259:260], -0.2, lamt[:, 258:259], ALU.add, ALU.subtract, (b_lamt,), (b_lamt,))
        neglam = lamt[:, 260:261]

        att = {'s': 0, 'pt': 0, 'c': 0}

        def s_bank():
            att['s'] += 1
            return (0, 1, 6, 7)[att['s'] % 4]

        def combine(tile_idx, h, o0, o1, ob0, ob1):
            cb_t, cb_b = ctmp[att['c'] % 2]
            att['c'] += 1
            tA = cb_t[:, 0:128]
            aT = cb_t[:, 128:256]
            zz = cb_t[:, 256:264]
            recip(zz[:, 0:1], o0[:, 128:129], (ob0,), (cb_b,))
            recip(zz[:, 1:2], o1[:, 128:129], (ob1,), (cb_b,))
            tt('dve', zz[:, 2:3], zz[:, 1:2], neglam, ALU.mult, (cb_b, b_lamt), (cb_b,))
            ts('dve', tA, o0[:, 0:128], zz[:, 0:1], None, ALU.mult, None, (ob0, cb_b), (cb_b,))
            stt('dve', aT, o1[:, 0:128], zz[:, 2:3], tA, ALU.mult, ALU.add, (ob1, cb_b), (cb_b,))
            act(tA, aT, AF.Square, (cb_b,), (cb_b,), accum_out=zz[:, 3:4])
            act(zz[:, 4:5], zz[:, 3:4], AF.Sqrt, (cb_b, b_eps), (cb_b,), bias=epsv[:, 0:1], scale=1.0 / 128)
            recip(zz[:, 5:6], zz[:, 4:5], (cb_b,), (cb_b,))
            stt('dve', attnO[:, tile_idx * 512 + h * 128:tile_idx * 512 + (h + 1) * 128], aT, zz[:, 5:6], gsub,
                ALU.mult, ALU.mult, (cb_b, b_gsub), (b_attnO,))

        OB = [(2, 3), (4, 5)]
        LOOK = 4
        pitems = [(sq, h, m, kb) for sq in range(2) for h in range(4) for m in range(2) for kb in range(2)]
        pst = {}

        def p_score(i):
            sq, h, m, kb = pitems[i]
            sb_ = s_bank()
            mm(PS[sb_][:, 0:256], kTp[64 * m:64 * m + 64, h * 512 + sq * 256 + kb * 128:h * 512 + sq * 256 + (kb + 1) * 128],
               qT[64 * m:64 * m + 64, h * 1024 + sq * 256:h * 1024 + (sq + 1) * 256], True, True, (b_kTp, b_qT), (PSB[sb_],))
            pt_t, pt_b = PTb[att['pt'] % 8]
            att['pt'] += 1
            act(pt_t[:, 0:256], PS[sb_][:, 0:256], AF.Exp, (PSB[sb_],), (pt_b,), scale=0.125)
            pst[i] = (pt_t, pt_b)

        for i in range(min(LOOK, len(pitems))):
            p_score(i)
        for i, (sq, h, m, kb) in enumerate(pitems):
            if i + LOOK < len(pitems):
                p_score(i + LOOK)
            oi_ = (sq * 4 + h) % 2
            ob = OB[m][oi_]
            if kb == 1:
                for qt in range(2):
                    for k2 in range(2):
                        pt_t, pt_b = pst[i - 1 + k2]
                        mm(PS[ob][:, qt * 129:qt * 129 + 129], pt_t[:, qt * 128:(qt + 1) * 128],
                           Vp[:, ((sq * 2 + k2) * 4 + h) * VW:((sq * 2 + k2) * 4 + h) * VW + 129], k2 == 0, k2 == 1,
                           (pt_b, b_Vp), (PSB[ob],))
                if m == 1:
                    for qt in range(2):
                        combine(sq * 2 + qt, h, PS[OB[0][oi_]][:, qt * 129:qt * 129 + 129], PS[OB[1][oi_]][:, qt * 129:qt * 129 + 129],
                                PSB[OB[0][oi_]], PSB[OB[1][oi_]])
        sitems = [(h, m, kb) for h in range(4) for m in range(2) for kb in range(18)]
        sst = {}

        def s_score(i):
            h, m, kb = sitems[i]
            sb_ = s_bank()
            mm(PS[sb_], kTs[64 * m:64 * m + 64, h * 2304 + kb * 128:h * 2304 + (kb + 1) * 128],
               qT[64 * m:64 * m + 64, h * 1024 + 512:h * 1024 + 1024], True, True, (b_kTs, b_qT), (PSB[sb_],))
            pt_t, pt_b = PTb[att['pt'] % 8]
            att['pt'] += 1
            act(pt_t, PS[sb_], AF.Exp, (PSB[sb_],), (pt_b,), scale=0.125)
            sst[i] = (pt_t, pt_b)

        for i in range(LOOK):
            s_score(i)
        for i, (h, m, kb) in enumerate(sitems):
            if i + LOOK < len(sitems):
                s_score(i + LOOK)
            obA, obB = OB[m]
            pt_t, pt_b = sst[i]
            for qt in range(4):
                ob = obA if qt < 3 else obB
                sl = qt if qt < 3 else 0
                first = (kb == 0) and (qt == 0 or qt == 3)
                mm(PS[ob][:, sl * 129:sl * 129 + 129], pt_t[:, qt * 128:(qt + 1) * 128],
                   Vs[:, (kb * 4 + h) * VW:(kb * 4 + h) * VW + 129], first, kb == 17, (pt_b, b_Vs), (PSB[ob],))
            if m == 1 and kb == 17:
                for qt in range(4):
                    sl = qt if qt < 3 else 0
                    i0 = 0 if qt < 3 else 1
                    combine(4 + qt, h, PS[OB[0][i0]][:, sl * 129:sl * 129 + 129], PS[OB[1][i0]][:, sl * 129:sl * 129 + 129],
                            PSB[OB[0][i0]], PSB[OB[1][i0]])
        if 'attnO' in dbg:
            dma('sp', dbg['attnO'], attnO, (b_attnO,), (), is_out=True)
        A.release(mark3)


        if stop_after == 13:
            S.finalize(nc, sems, dsems, None)
            return nc
        A.release(markC)
        if not bisect.get('ssm', True):
            memset('dve', zT, 0.0, (b_zT,))
        else:
            NSL = 32
            g0, b_g0 = A.alloc("g0", 32 * 24)
            def G0(k):
                return g0[:, k * 32:(k + 1) * 32]
            (LR, LI, LST, H0R, H0I, LRC, STEP, XLOG, UTR, TMPA, TMPB, NR, DEN, FRE, FIM, A16R, A16I, RHO, TMPC, TMPD) = range(20)
            rowst, b_rowst = A.alloc("rowst", 5 * 128)
            Bre, b_Bre = A.alloc("Bre", 512)
            Bim, b_Bim = A.alloc("Bim", 512)
            Cre, b_Cre = A.alloc("Cre", 512)
            Cim, b_Cim = A.alloc("Cim", 512)
            BBre, b_BBre = A.alloc("BBre", 512)
            BBim, b_BBim = A.alloc("BBim", 512)
            PWre, b_PWre = A.alloc("PWre", 17 * 32)
            PWim, b_PWim = A.alloc("PWim", 17 * 32)
            MAG, b_MAG = A.alloc("MAG", 17 * 32)
            Rc, b_Rc = A.alloc("Rc", 1024)
            Rs, b_Rs = A.alloc("Rs", 1024)
            dT, b_dT = A.alloc("dT", 12)
            FIN, b_FIN = A.alloc("FIN", 2 * 2 * 32)
            crow = [A.alloc(f"crow{i}", 128) for i in range(1)]
            wk = [AW.alloc("wk0", 1088), AW.alloc("wk1", 1088), AW.alloc("wk2", 512)]
            CAf = [AW.alloc(f"CAf{i}", 17 * 64) for i in range(2)]
            dw = [AW.alloc(f"dw{i}", 160) for i in range(4)]
            for k_, src in enumerate([s_lre, s_lim, None, h0re, h0im]):
                if src is None:
                    dma('sp', crow[0][0][0:32, 0:2], s_lstep[:, :], (), (crow[0][1],))
                    cp('dve', V(rowst, k_ * 128, [(64, 2), (1, 64)])[0:32], V(crow[0][0], 0, [(1, 2), (0, 64)])[0:32],
                       (crow[0][1],), (b_rowst,))
                else:
                    dma('sp', rowst[0:32, k_ * 128:(k_ + 1) * 128], src[:, :], (), (b_rowst,))
            dma('sp', V(Bre, 0, [(16, 32), (1, 16)]), s_bre.rearrange("d (gp g2) p c -> (g2 p) (d gp) c", g2=2), (), (b_Bre,), nc_ok=True)
            dma('sp', V(Bim, 0, [(16, 32), (1, 16)]), s_bim.rearrange("d (gp g2) p c -> (g2 p) (d gp) c", g2=2), (), (b_Bim,), nc_ok=True)
            for d_ in range(2):
                dma('sp', V(dT, d_ * 4, [(1, 4)]), s_d[d_, :].rearrange("(j p) -> p j", p=128), (), (b_dT,), nc_ok=True)
            for k_ in range(5):
                tp(PS[6][:, k_ * 32:(k_ + 1) * 32], rowst[0:32, k_ * 128:(k_ + 1) * 128], ident[0:32, 0:32], (b_rowst, b_ident), (PSB[6],))
            cp('dve', g0[:, 0:160], PS[6][:, 0:160], (PSB[6],), (b_g0,))
            mk_craw = AW.mark()
            craw = AW.alloc("craw", 2048)
            for (Cdst, b_Cdst, csrc, pbC) in ((Cre, b_Cre, s_cre, 7), (Cim, b_Cim, s_cim, 5)):
                cr_t, cr_b = craw
                for g2 in range(2):
                    src = bass.AP(tensor=csrc.tensor, offset=csrc.offset + g2 * 1024, ap=[[2048, 32], [64, 16], [1, 64]])
                    dma('sp', V(cr_t, g2 * 64, [(128, 16), (1, 64)])[0:32], src, (), (cr_b,))
                for c_ in range(16):
                    tp(PS[pbC][:, c_ * 32:(c_ + 1) * 32], cr_t[0:32, c_ * 128:(c_ + 1) * 128], ident[0:32, 0:32],
                       (cr_b, b_ident), (PSB[pbC],))
                cp('dve', V(Cdst, 0, [(1, 16), (16, 32)]), V(PS[pbC], 0, [(32, 16), (1, 32)]), (PSB[pbC],), (b_Cdst,))
            AW.release(mk_craw)
            gb = (b_g0,)
            ts('dve', G0(LRC), G0(LR), -1e-4, None, ALU.min, None, gb, gb)
            act(G0(STEP), G0(LST), AF.Exp, gb, gb)
            tt('dve', G0(XLOG), G0(LRC), G0(STEP), ALU.mult, gb, gb)
            tt('dve', G0(TMPA), G0(LI), G0(STEP), ALU.mult, gb, gb)
            ts('dve', G0(TMPA), G0(TMPA), 1.0 / TWO_PI, None, ALU.mult, None, gb, gb)
            ts('dve', G0(TMPB), G0(TMPA), MAGIC, None, ALU.add, None, gb, gb)
            ts('dve', G0(TMPB), G0(TMPB), -MAGIC, None, ALU.add, None, gb, gb)
            tt('dve', G0(UTR), G0(TMPA), G0(TMPB), ALU.subtract, gb, gb)

            def sincos(dst_c, dst_s, bc, bs, yv, n, wka, wkb):
                ra, rab = wka
                rb, rbb = wkb
                ts('dve', ra[:, 0:n], yv, MAGIC, None, ALU.add, None, (wk[0][1],), (rab,))
                ts('dve', ra[:, 0:n], ra[:, 0:n], -MAGIC, None, ALU.add, None, (rab,), (rab,))
                tt('dve', ra[:, 0:n], yv, ra[:, 0:n], ALU.subtract, (wk[0][1], rab), (rab,))
                act(dst_s, ra[:, 0:n], AF.Sin, (rab,), (bs,), scale=TWO_PI)
                ts('dve', rb[:, 0:n], yv, 0.25, None, ALU.add, None, (wk[0][1],), (rbb,))
                ts('dve', ra[:, 0:n], rb[:, 0:n], MAGIC, None, ALU.add, None, (rbb,), (rab,))
                ts('dve', ra[:, 0:n], ra[:, 0:n], -MAGIC, None, ALU.add, None, (rab,), (rab,))
                tt('dve', ra[:, 0:n], rb[:, 0:n], ra[:, 0:n], ALU.subtract, (rbb, rab), (rab,))
                act(dst_c, ra[:, 0:n], AF.Sin, (rab,), (bc,), scale=TWO_PI)

            mt17 = V(ctab, 0, [(1, 17), (0, 32)])
            tt('dve', V(MAG, 0, [(32, 17), (1, 32)]), V(g0, XLOG * 32, [(0, 17), (1, 32)]), mt17, ALU.mult, (b_g0, b_ctab), (b_MAG,))
            act(MAG, MAG, AF.Exp, (b_MAG,), (b_MAG,))
            tt('dve', V(wk[0][0], 0, [(32, 17), (1, 32)]), V(g0, UTR * 32, [(0, 17), (1, 32)]), mt17, ALU.mult, (b_g0, b_ctab), (wk[0][1],))
            sincos(PWre, PWim, b_PWre, b_PWim, wk[0][0][:, 0:544], 544, CAf[0], CAf[1])
            tt('dve', PWre, PWre, MAG, ALU.mult, (b_PWre, b_MAG), (b_PWre,))
            tt('dve', PWim, PWim, MAG, ALU.mult, (b_PWim, b_MAG), (b_PWim,))
            tt('dve', V(wk[0][0], 0, [(32, 32), (1, 32)]), V(g0, UTR * 32, [(1, 32), (0, 32)]), V(ctab, 17, [(0, 32), (1, 32)]),
               ALU.mult, (b_g0, b_ctab), (wk[0][1],))
            sincos(Rc, Rs, b_Rc, b_Rs, wk[0][0][:, 0:1024], 1024, CAf[0], CAf[1])
            P1r = PWre[:, 32:64]
            P1i = PWim[:, 32:64]
            ts('dve', G0(NR), P1r, -1.0, None, ALU.add, None, (b_PWre,), gb)
            tt('dve', G0(DEN), G0(LRC), G0(LRC), ALU.mult, gb, gb)
            tt('dve', G0(TMPA), G0(LI), G0(LI), ALU.mult, gb, gb)
            tt('dve', G0(DEN), G0(DEN), G0(TMPA), ALU.add, gb, gb)
            recip(G0(DEN), G0(DEN), gb, gb)
            tt('dve', G0(TMPA), G0(NR), G0(LRC), ALU.mult, gb, gb)
            tt('dve', G0(TMPB), P1i, G0(LI), ALU.mult, (b_PWim, b_g0), gb)
            tt('dve', G0(TMPA), G0(TMPA), G0(TMPB), ALU.add, gb, gb)
            tt('dve', G0(FRE), G0(TMPA), G0(DEN), ALU.mult, gb, gb)
            tt('dve', G0(TMPA), P1i, G0(LRC), ALU.mult, (b_PWim, b_g0), gb)
            tt('dve', G0(TMPB), G0(NR), G0(LI), ALU.mult, gb, gb)
            tt('dve', G0(TMPA), G0(TMPA), G0(TMPB), ALU.subtract, gb, gb)
            tt('dve', G0(FIM), G0(TMPA), G0(DEN), ALU.mult, gb, gb)
            def bc16(k):
                return V(g0, k * 32, [(1, 32), (0, 16)])
            B3 = lambda t: V(t, 0, [(16, 32), (1, 16)])
            w0, w0b = wk[0]
            w1, w1b = wk[1]
            tt('dve', B3(w0), bc16(FRE), B3(Bre), ALU.mult, (b_g0, b_Bre), (w0b,))
            tt('dve', B3(w1), bc16(FIM), B3(Bim), ALU.mult, (b_g0, b_Bim), (w1b,))
            tt('dve', B3(BBre), B3(w0), B3(w1), ALU.subtract, (w0b, w1b), (b_BBre,))
            tt('dve', B3(w0), bc16(FRE), B3(Bim), ALU.mult, (b_g0, b_Bim), (w0b,))
            tt('dve', B3(w1), bc16(FIM), B3(Bre), ALU.mult, (b_g0, b_Bre), (w1b,))
            tt('dve', B3(BBim), B3(w0), B3(w1), ALU.add, (w0b, w1b), (b_BBim,))
            cp('dve', G0(A16R), PWre[:, 16 * 32:17 * 32], (b_PWre,), gb)
            cp('dve', G0(A16I), PWim[:, 16 * 32:17 * 32], (b_PWim,), gb)
            cp('dve', G0(RHO), MAG[:, 16 * 32:17 * 32], (b_MAG,), gb)
            tt('dve', dT[:, 8:12], dT[:, 0:4], dT[:, 4:8], ALU.add, (b_dT,), (b_dT,))
            memset('dve', FIN, 0.0, (b_FIN,))
            AHE, b_AHE = A.alloc("AHE", 4 * 32)
            def cmul32(dre, dim, ar, ai, br, bi, rd):
                tt('dve', G0(TMPA), ar, br, ALU.mult, rd, gb)
                tt('dve', G0(TMPB), ai, bi, ALU.mult, rd, gb)
                tt('dve', dre, G0(TMPA), G0(TMPB), ALU.subtract, gb, (b_AHE,))
                tt('dve', G0(TMPA), ar, bi, ALU.mult, rd, gb)
                tt('dve', G0(TMPB), ai, br, ALU.mult, rd, gb)
                tt('dve', dim, G0(TMPA), G0(TMPB), ALU.add, gb, (b_AHE,))
            cmul32(AHE[:, 0:32], AHE[:, 32:64], G0(A16R), G0(A16I), G0(H0R), G0(H0I), (b_g0,))
            cmul32(AHE[:, 64:96], AHE[:, 96:128], G0(A16R), G0(A16I), V(Rc, 31, [(32, 32)]), V(Rs, 31, [(32, 32)]), (b_g0, b_Rc, b_Rs))

            Yst = [A.alloc(f"Yst{i}", 4 * 2 * 2 * 128, BF16) for i in range(2)]
            BT, b_BT = A.alloc("BT", 16 * 2 * 2 * 128, BF16)
            RHl, b_RHl = AW.alloc("RHl", 4 * 2 * 512, BF16)
            RHcs = [A.alloc(f"RHc{i}", 4 * 2 * 512, BF16) for i in range(2)]
            BBblk, b_BBblk = A.alloc("BBblk", 4 * 2 * 128, BF16)
            Kbds = [A.alloc(f"Kbd{i}", 33 * 128, BF16) for i in range(2)]
            Gt, b_Gt = A.alloc("Gt", 2 * 5 * 128)
            GSt, b_GSt = A.alloc("GSt", 2 * 5 * 128)
            TC, b_TC = A.alloc("TC", 5 * 128)
            TS, b_TS = A.alloc("TS", 5 * 128)
            D0, b_D0 = AW.alloc("D0", 2 * 128)
            Hbuf, b_Hbuf = A.alloc("Hbuf", 2 * 2 * 4 * 64, BF16)
            Hr, b_Hr = AW.alloc("Hr", 2 * 128)
            sml, b_sml = AW.alloc("sml", 64)
            Ysb, b_Ysb = A.alloc("Ysb", 2 * 4 * 512, BF16)
            Dsk, b_Dsk = A.alloc("Dsk", 128, BF16)
            for t_, b_ in ((Yst[0][0], Yst[0][1]), (Yst[1][0], Yst[1][1]), (RHl, b_RHl), (RHcs[0][0], RHcs[0][1]), (RHcs[1][0], RHcs[1][1]), (BBblk, b_BBblk), (Hbuf, b_Hbuf)):
                memset('pool', t_, 0.0, (b_,))
            ssm_ps = {'n': 0}

            def sps():
                ssm_ps['n'] += 1
                return 2 + ssm_ps['n'] % 4

            def gen_unit(j, dr):
                s0 = dr * 16 + 4 * j
                u = j * 2 + dr
                RHc, b_RHc = RHcs[u % 2]
                Kbd, b_Kbd = Kbds[j % 2]
                (w0, w0b), (w1, w1b) = wk[0], wk[1]
                first = 0 if dr == 0 else 31
                pend_ev = []
                for th in range(4):
                    ys_t, ys_b = Yst[th % 2]
                    if dr == 0:
                        e0, es = 15 - th * 4, -32
                    else:
                        e0, es = th * 4, 32
                    pwr = V(PWre, e0 * 32 + s0, [(es, 4), (1, 4), (0, 16)])
                    pwi = V(PWim, e0 * 32 + s0, [(es, 4), (1, 4), (0, 16)])
                    bbr = V(BBre, s0 * 16, [(0, 4), (16, 4), (1, 16)])
                    bbi = V(BBim, s0 * 16, [(0, 4), (16, 4), (1, 16)])
                    X3 = lambda t: V(t, 0, [(64, 4), (16, 4), (1, 16)])
                    w2, w2b = wk[2]
                    for ri in range(2):
                        if ri == 0:
                            tt('pool', X3(w0), pwr, bbr, ALU.mult, (b_PWre, b_BBre), (w0b,))
                            tt('pool', X3(w1), pwi, bbi, ALU.mult, (b_PWim, b_BBim), (w1b,))
                            tt('pool', X3(w2), X3(w0), X3(w1), ALU.subtract, (w0b, w1b), (w2b,))
                        else:
                            tt('pool', X3(w0), pwr, bbi, ALU.mult, (b_PWre, b_BBim), (w0b,))
                            tt('pool', X3(w1), pwi, bbr, ALU.mult, (b_PWim, b_BBre), (w1b,))
                            tt('pool', X3(w2), X3(w0), X3(w1), ALU.add, (w0b, w1b), (w2b,))
                        for par in range(2):
                            for g2 in range(2):
                                cp('act', V(ys_t, ri * 256 + par * 128 + par * 32 + g2 * 16, [(512, 4), (64, 2), (1, 16)])[g2 * 64:(g2 + 1) * 64],
                                   V(w2, par * 16, [(64, 4), (32, 2), (1, 16)])[g2 * 64:(g2 + 1) * 64], (w2b,), (ys_b,))
                    for (bt_sl, pbk) in pend_ev:
                        cp('act', bt_sl, PSH[pbk], (PSB[pbk],), (b_BT,))
                    pend_ev.clear()
                    pbt = next_tps()
                    for t8 in range(4):
                        for ri in range(2):
                            for par in range(2):
                                col = ((t8 % 2) * 4 + ri * 2 + par) * 128
                                tp(PSH[pbt][:, col:col + 128], ys_t[:, (t8 * 4 + ri * 2 + par) * 128:(t8 * 4 + ri * 2 + par + 1) * 128],
                                   identb, (ys_b, b_identb), (PSB[pbt],))
                        if t8 % 2 == 1:
                            tau0 = th * 4 + t8 - 1
                            pend_ev.append((BT[:, tau0 * 512:(tau0 + 2) * 512], pbt))
                            pbt = next_tps()
                for (bt_sl, pbk) in pend_ev:
                    cp('act', bt_sl, PSH[pbk], (PSB[pbk],), (b_BT,))
                pend_ev.clear()
                pw_r = V(PWre, s0, [(32, 17), (1, 4), (0, 16)])
                pw_i = V(PWim, s0, [(32, 17), (1, 4), (0, 16)])
                c_r = V(Cre, s0 * 16, [(0, 17), (16, 4), (1, 16)])
                c_i = V(Cim, s0 * 16, [(0, 17), (16, 4), (1, 16)])
                W3 = lambda t: V(t, 0, [(64, 17), (16, 4), (1, 16)])
                tt('pool', W3(w0), pw_r, c_r, ALU.mult, (b_PWre, b_Cre), (w0b,))
                tt('pool', W3(w1), pw_i, c_i, ALU.mult, (b_PWim, b_Cim), (w1b,))
                tt('pool', W3(CAf[0][0]), W3(w0), W3(w1), ALU.subtract, (w0b, w1b), (CAf[0][1],))
                tt('pool', W3(w0), pw_i, c_r, ALU.mult, (b_PWim, b_Cre), (w0b,))
                tt('pool', W3(w1), pw_r, c_i, ALU.mult, (b_PWre, b_Cim), (w1b,))
                tt('pool', W3(w0), W3(w0), W3(w1), ALU.add, (w0b, w1b), (w0b,))
                ts('pool', W3(CAf[1][0]), W3(w0), -1.0, None, ALU.mult, None, (w0b,), (CAf[1][1],))
                for ri, BBt, bBB in ((0, BBre, b_BBre), (1, BBim, b_BBim)):
                    for g2 in range(2):
                        cp('act', V(BBblk, ri * 128 + g2 * 16, [(256 + 32, 4), (1, 16)])[g2 * 64:(g2 + 1) * 64],
                           V(BBt, s0 * 16, [(16, 4), (1, 16)])[g2 * 64:(g2 + 1) * 64], (bBB,), (b_BBblk,))
                for ri in range(2):
                    for g2 in range(2):
                        cp('act', V(RHl, ri * 512 + g2 * 16, [(1024, 4), (32, 16), (1, 16)])[g2 * 64:(g2 + 1) * 64],
                           V(CAf[ri][0], 0, [(16, 4), (64, 16), (1, 16)])[g2 * 64:(g2 + 1) * 64], (CAf[ri][1],), (b_RHl,))
                for r in range(4):
                    pb = sps()
                    for ri in range(2):
                        mm(PS[pb], BBblk[:, (r * 2 + ri) * 128:(r * 2 + ri + 1) * 128], RHl[:, (r * 2 + ri) * 512:(r * 2 + ri + 1) * 512],
                           ri == 0, ri == 1, (b_BBblk, b_RHl), (PSB[pb],))
                    cp('act', V(Kbd, dr * 16 * 128 + r * 32, [(128, 16), (1, 32)]), V(PS[pb], 0, [(32, 16), (1, 32)]),
                       (PSB[pb],), (b_Kbd,))
                for ri in range(2):
                    for g2 in range(2):
                        if dr == 0:
                            src = V(CAf[ri][0], 64, [(16, 4), (64, 16), (1, 16)])
                        else:
                            src = V(CAf[ri][0], 16 * 64, [(16, 4), (-64, 16), (1, 16)])
                        cp('act', V(RHc, ri * 512 + g2 * 16, [(1024, 4), (32, 16), (1, 16)])[g2 * 64:(g2 + 1) * 64],
                           src[g2 * 64:(g2 + 1) * 64], (CAf[ri][1],), (b_RHc,))

            def head_unit(j, dr):
                s0 = dr * 16 + 4 * j
                u = j * 2 + dr
                RHc, b_RHc = RHcs[u % 2]
                Kbd, b_Kbd = Kbds[j % 2]
                (w0, w0b), (w1, w1b) = wk[0], wk[1]
                first = 0 if dr == 0 else 31
                for sl in range(5):
                    if sl == 0:
                        if dr == 0:
                            srcc = V(Rc, s0 * 32, [(32, 4), (0, 2), (1, 16)])
                            srcs = V(Rs, s0 * 32, [(32, 4), (0, 2), (1, 16)])
                        else:
                            srcc = V(Rc, s0 * 32 + 15, [(32, 4), (0, 2), (-1, 16)])
                            srcs = V(Rs, s0 * 32 + 15, [(32, 4), (0, 2), (-1, 16)])
                        dstc = V(TC, sl * 128, [(32, 4), (16, 2), (1, 16)])
                        dsts = V(TS, sl * 128, [(32, 4), (16, 2), (1, 16)])
                    else:
                        if dr == 0:
                            srcc = V(Rc, s0 * 32, [(32, 4), (1, 32)])
                            srcs = V(Rs, s0 * 32, [(32, 4), (1, 32)])
                        else:
                            srcc = V(Rc, s0 * 32 + 31, [(32, 4), (-1, 32)])
                            srcs = V(Rs, s0 * 32 + 31, [(32, 4), (-1, 32)])
                        dstc = V(TC, sl * 128, [(32, 4), (1, 32)])
                        dsts = V(TS, sl * 128, [(32, 4), (1, 32)])
                    if sl >= 2:
                        fl = flags[:, dr * 3 + (sl - 2):dr * 3 + (sl - 2) + 1]
                        act(dstc, srcc, AF.Copy, (b_Rc, b_flags), (b_TC,), scale=fl)
                        act(dsts, srcs, AF.Copy, (b_Rs, b_flags), (b_TS,), scale=fl)
                    else:
                        cp('act', dstc, srcc, (b_Rc,), (b_TC,))
                        cp('act', dsts, srcs, (b_Rs,), (b_TS,))
                for ty in range(2):
                    cp('pool', V(D0, ty * 128, [(32, 4), (1, 32)]), V(g0, RHO * 32 + s0, [(1, 4), (0, 32)]), (b_g0,), (b_D0,))
                first = 0 if dr == 0 else 31
                memset('pool', V(D0, 128 + first, [(32, 4)]), 0.0, (b_D0,))
                memset('pool', V(D0, first if dr == 0 else 15, [(16, 8)]), 0.0, (b_D0,))
                for r in range(4):
                    pb = sps()
                    hp = 64 * (r // 2)
                    for ri in range(2):
                        for tau in range(16):
                            mm(PS[pb][:, ri * 160:(ri + 1) * 160],
                               BT[hp:hp + 64, (tau * 4 + ri * 2 + (r % 2)) * 128:(tau * 4 + ri * 2 + (r % 2) + 1) * 128],
                               uT[hp:hp + 64, j * NTOK_ALL + tau * 160:j * NTOK_ALL + (tau + 1) * 160], tau == 0, tau == 15, (b_BT, b_uT), (PSB[pb],))
                    s_re = V(PS[pb], 0, [(32, 5), (1, 32)])
                    s_im = V(PS[pb], 160, [(32, 5), (1, 32)])
                    tc_ = V(TC, r * 32, [(128, 5), (1, 32)])
                    ts_ = V(TS, r * 32, [(128, 5), (1, 32)])
                    g_re = V(Gt, r * 32, [(128, 5), (1, 32)])
                    g_im = V(Gt, 640 + r * 32, [(128, 5), (1, 32)])
                    S5v = lambda t: V(t, 0, [(32, 5), (1, 32)])
                    (w0, w0b), (w1, w1b), (w2_, w2b_), (w3_, w3b_) = dw
                    tt('dve', S5v(w0), s_re, tc_, ALU.mult, (PSB[pb], b_TC), (w0b,))
                    tt('dve', S5v(w1), s_im, ts_, ALU.mult, (PSB[pb], b_TS), (w1b,))
                    tt('dve', S5v(w2_), s_im, tc_, ALU.mult, (PSB[pb], b_TC), (w2b_,))
                    tt('dve', S5v(w3_), s_re, ts_, ALU.mult, (PSB[pb], b_TS), (w3b_,))
                    tt('dve', g_re, S5v(w0), S5v(w1), ALU.add, (w0b, w1b), (b_Gt,))
                    tt('dve', g_im, S5v(w2_), S5v(w3_), ALU.subtract, (w2b_, w3b_), (b_Gt,))

            def tail_unit(j, dr):
                s0 = dr * 16 + 4 * j
                u = j * 2 + dr
                RHc, b_RHc = RHcs[u % 2]
                Kbd, b_Kbd = Kbds[j % 2]
                (w0, w0b), (w1, w1b) = wk[0], wk[1]
                first = 0 if dr == 0 else 31
                (w0, w0b), (w1, w1b) = wk[0], wk[1]
                first = 0 if dr == 0 else 31
                sm_ = lambda a: sml[:, a:a + 4]
                a16r = g0[:, A16R * 32 + s0:A16R * 32 + s0 + 4]
                a16i = g0[:, A16I * 32 + s0:A16I * 32 + s0 + 4]
                h0r_ = g0[:, H0R * 32 + s0:H0R * 32 + s0 + 4]
                h0i_ = g0[:, H0I * 32 + s0:H0I * 32 + s0 + 4]
                sb_ = (b_sml,)

                def cmul(dre, dim, ar, ai, br, bi, rd):
                    tt('dve', sm_(8), ar, br, ALU.mult, rd, sb_)
                    tt('dve', sm_(40), ar, bi, ALU.mult, rd, sb_)
                    tt('dve', sm_(12), ai, bi, ALU.mult, rd, sb_)
                    tt('dve', sm_(44), ai, br, ALU.mult, rd, sb_)
                    tt('dve', dre, sm_(8), sm_(12), ALU.subtract, sb_, sb_)
                    tt('dve', dim, sm_(40), sm_(44), ALU.add, sb_, sb_)
                lastpos = 31 if dr == 0 else 0
                rc31 = V(Rc, s0 * 32 + 31, [(32, 4)])
                rs31 = V(Rs, s0 * 32 + 31, [(32, 4)])
                order = [2, 3, 4, 1] if dr == 0 else [4, 3, 2, 1]
                inj_idx = {2: 0, 3: 1, 4: 2, 1: 3}
                d0P = D0[:, 0:128]
                d0S = D0[:, 128:256]

                def gslot(t, ri, sl):
                    return t[:, (ri * 5 + sl) * 128:(ri * 5 + sl + 1) * 128]

                def do_scan(sl):
                    d0 = d0P if sl == 0 else d0S
                    for ri in range(2):
                        if dr == 0:
                            scan(gslot(GSt, ri, sl), d0, gslot(Gt, ri, sl), (b_D0, b_Gt), (b_GSt,))
                        else:
                            scan(gslot(GSt, ri, sl)[:, ::-1], d0[:, ::-1], gslot(Gt, ri, sl)[:, ::-1], (b_D0, b_Gt), (b_GSt,))
                do_scan(0)
                for oi, sl in enumerate(order):
                    fl = flags[:, 6 + dr * 4 + inj_idx[sl]:6 + dr * 4 + inj_idx[sl] + 1]
                    for ri in range(2):
                        gf = V(Gt, (ri * 5 + sl) * 128 + first, [(32, 4)])
                        stt('dve', gf, AHE[:, ri * 32 + s0:ri * 32 + s0 + 4], fl, gf, ALU.mult, ALU.add, (b_AHE, b_flags, b_Gt), (b_Gt,))
                    if oi > 0:
                        psl = order[oi - 1]
                        glr = V(GSt, (0 * 5 + psl) * 128 + lastpos, [(32, 4)])
                        gli = V(GSt, (1 * 5 + psl) * 128 + lastpos, [(32, 4)])
                        if sl == 1:
                            cmul(sm_(24), sm_(28), rc31, rs31, glr, gli, (b_Rc, b_Rs, b_GSt, b_sml))
                            cmul(sm_(32), sm_(36), a16r, a16i, sm_(24), sm_(28), (b_g0, b_sml))
                        else:
                            cmul(sm_(32), sm_(36), AHE[:, 64 + s0:64 + s0 + 4], AHE[:, 96 + s0:96 + s0 + 4], glr, gli, (b_GSt, b_sml, b_AHE))
                        for ri in range(2):
                            gf = V(Gt, (ri * 5 + sl) * 128 + first, [(32, 4)])
                            tt('dve', gf, gf, sm_(32 + 4 * ri), ALU.add, (b_Gt, b_sml), (b_Gt,))
                    do_scan(sl)
                flo = flags[:, 6 + dr * 4 + 3:6 + dr * 4 + 4]
                stt('dve', sm_(24), h0r_, flo, sm_(24), ALU.mult, ALU.add, (b_g0, b_flags, b_sml), sb_)
                stt('dve', sm_(28), h0i_, flo, sm_(28), ALU.mult, ALU.add, (b_g0, b_flags, b_sml), sb_)
                hb0 = dr * (2 * 4 * 64)
                (w0, w0b), (w1, w1b) = dw[0], dw[1]
                for sl in (0, 1):
                    tcs = TC[:, sl * 128:(sl + 1) * 128]
                    tss = TS[:, sl * 128:(sl + 1) * 128]
                    gr = gslot(GSt, 0, sl)
                    gi = gslot(GSt, 1, sl)
                    for ri in range(2):
                        if ri == 0:
                            tt('dve', w0[:, 0:128], tcs, gr, ALU.mult, (b_TC, b_GSt), (w0b,))
                            tt('dve', w1[:, 0:128], tss, gi, ALU.mult, (b_TS, b_GSt), (w1b,))
                            tt('dve', Hr[:, 0:128], w0[:, 0:128], w1[:, 0:128], ALU.subtract, (w0b, w1b), (b_Hr,))
                        else:
                            tt('dve', w0[:, 0:128], tss, gr, ALU.mult, (b_TS, b_GSt), (w0b,))
                            tt('dve', w1[:, 0:128], tcs, gi, ALU.mult, (b_TC, b_GSt), (w1b,))
                            tt('dve', Hr[:, 0:128], w0[:, 0:128], w1[:, 0:128], ALU.add, (w0b, w1b), (b_Hr,))
                        hoff = hb0 + ri * (4 * 64)
                        sh = 1 if dr == 0 else 0
                        if sl == 0:
                            cp('dve', V(Hbuf, hoff + sh, [(64, 4), (16, 2), (1, 15)]), V(Hr, 1 - sh, [(32, 4), (16, 2), (1, 15)]), (b_Hr,), (b_Hbuf,))
                            fpos = 15 if dr == 0 else 0
                            cp('dve', V(FIN, ri * 64 + s0, [(32, 2), (1, 4)]), V(Hr, fpos, [(16, 2), (32, 4)]), (b_Hr,), (b_FIN,))
                        else:
                            cp('dve', V(Hbuf, hoff + 32 + sh, [(64, 4), (1, 31)]), V(Hr, 1 - sh, [(32, 4), (1, 31)]), (b_Hr,), (b_Hbuf,))
                            ipos = 32 if dr == 0 else 63
                            cp('dve', V(Hbuf, hoff + ipos, [(64, 4)]), sm_(24 + 4 * ri), (b_sml,), (b_Hbuf,))
                (w0, w0b), (w1, w1b) = wk[0], wk[1]
                first = 0 if dr == 0 else 31
                for blk in range(2):
                    for r in range(4):
                        pb = sps()
                        for ri in range(2):
                            hoff = hb0 + ri * (4 * 64) + r * 64 + blk * 32
                            lh = Hbuf[:, hoff:hoff + 32]
                            mm(PS[pb][0:32, :], lh, RHc[:, (r * 2 + ri) * 512:(r * 2 + ri + 1) * 512], ri == 0, ri == 1,
                               (b_Hbuf, b_RHc), (PSB[pb],))
                        yv = V(Ysb, blk * 2048 + r * 32, [(128, 16), (1, 32)])[0:32]
                        if dr == 0:
                            cp('act', yv, V(PS[pb], 0, [(32, 16), (1, 32)])[0:32], (PSB[pb],), (b_Ysb,))
                        else:
                            tt('dve', yv, V(PS[pb], 0, [(32, 16), (1, 32)])[0:32], yv, ALU.add, (PSB[pb], b_Ysb), (b_Ysb,))

            def yacc(j):
                Kbd, b_Kbd = Kbds[j % 2]
                for blk in range(2):
                    pby = 6 + blk
                    ub = j * 1024 + blk * 512
                    mm(PS[pby], Dsk, uTA[:, ub:ub + 512], True, False, (b_Dsk, b_uTA), (PSB[pby],))
                    for l in range(16):
                        n_ = (16 - l) * 32
                        mm(PS[pby][:, l * 32:512], Kbd[:, l * 128:(l + 1) * 128], uTA[:, ub:ub + n_],
                           False, False, (b_Kbd, b_uTA), (PSB[pby],))
                        mm(PS[pby][:, 0:n_], Kbd[:, (16 + l) * 128:(17 + l) * 128], uTA[:, ub + l * 32:ub + 512],
                           False, False, (b_Kbd, b_uTA), (PSB[pby],))
                    for tau in range(16):
                        mm(PS[pby][:, tau * 32:(tau + 1) * 32], Ysb[0:32, blk * 2048 + tau * 128:blk * 2048 + (tau + 1) * 128], identb[0:32, 0:32],
                           False, tau == 15, (b_Ysb, b_identb), (PSB[pby],))
                    act(V(zT, j * 1024 + blk * 512, [(1, 16), (16, 32)]), V(PS[pby], 0, [(32, 16), (1, 32)]), AF.Gelu_apprx_tanh, (PSB[pby],), (b_zT,))
                    if 'yssm' in dbg:
                        cp('dve', V(wk[0][0], 0, [(1, 16), (16, 32)]), V(PS[pby], 0, [(32, 16), (1, 32)]), (PSB[pby],), (wk[0][1],))
                        dma('sp', dbg['yssm'][:, j * 1024 + blk * 512:j * 1024 + (blk + 1) * 512], wk[0][0][:, 0:512], (wk[0][1],), (), is_out=True)

            gen_unit(0, 0)
            for j in range(4):
                ts('dve', Dsk, identb, dT[:, 8 + j:9 + j], None, ALU.mult, None, (b_identb, b_dT), (b_Dsk,))
                for dr in range(2):
                    u = j * 2 + dr
                    head_unit(j, dr)
                    if u + 1 < 8:
                        gen_unit((u + 1) // 2, (u + 1) % 2)
                    tail_unit(j, dr)
                yacc(j)
            for ri, dst in ((0, o_sre), (1, o_sim)):
                for sq in range(2):
                    pb = sps()
                    tp(PS[pb][0:32, 0:128], FIN[:, ri * 64 + sq * 32:ri * 64 + (sq + 1) * 32], ident, (b_FIN, b_ident), (PSB[pb],))
                    cp('dve', rowst[0:32, (ri * 2 + sq) * 128:(ri * 2 + sq + 1) * 128], PS[pb][0:32, 0:128], (PSB[pb],), (b_rowst,))
                    dma('sp', dst[sq, :, :], rowst[0:32, (ri * 2 + sq) * 128:(ri * 2 + sq + 1) * 128], (b_rowst,), (), is_out=True)
        A.release(markB)
        AW.release(0)

        if stop_after == 14:
            S.finalize(nc, sems, dsems, None)
            return nc
        grow, b_grow = A.alloc("grow", 2 * 2 * D)
        mkg = A.mark()
        ones32, b_ones32 = A.alloc("ones32", 128)
        GG, b_GG = A.alloc("GG", 32)
        Dg = [A.alloc(f"Dg{i}", 128) for i in range(4)]
        memset('dve', ones32, 1.0, (b_ones32,))
        for g2_, (comp, gi_) in enumerate([(2, 1), (5, 3)]):
            tt('dve', V(GG, g2_ * 16, [(2, 8), (1, 2)]), V(modT, comp * 16, [(2, 8), (1, 2)]), V(ngT, gi_ * 8, [(1, 8), (0, 2)]),
               ALU.mult, (b_modT, b_ngT), (b_GG,))
        kk = 0
        for g2_ in range(2):
            for c in range(2):
                for half in range(2):
                    pb = next_ps()
                    for k4 in range(4):
                        k_ = half * 4 + k4
                        dg_t, dg_b = Dg[kk % 4]
                        kk += 1
                        ts('dve', dg_t, ident, GG[:, g2_ * 16 + k_ * 2 + c:g2_ * 16 + k_ * 2 + c + 1], None, ALU.mult, None,
                           (b_ident, b_GG), (dg_b,))
                        mm(PS[pb][:, k4 * 128:(k4 + 1) * 128], ones32, dg_t, True, True, (b_ones32, dg_b), (PSB[pb],))
                    cp(ev_eng(), grow[:, (g2_ * 2 + c) * D + half * 512:(g2_ * 2 + c) * D + half * 512 + 512], PS[pb], (PSB[pb],), (b_grow,))
        A.release(mkg)
        x1 = [A.alloc(f"x1_{i}", D) for i in range(8)]
        wfo = [A.alloc(f"wfo{j}", 1024, BF16) for j in range(22)]
        mark5 = A.mark()
        mixT, b_mixT = A.alloc("mixT", 8 * 1024, BF16)
        wo = [A.alloc(f"wo{c}", 1024, BF16) for c in range(8)]
        wglu = [A.alloc(f"wglu{c}", 512, BF16) for c in range(4)]
        bgluT, b_bgluT = A.alloc("bgluT", 4)
        sig = [A.alloc(f"sig{i}", 512, BF16) for i in range(2)]
        xst5 = [A.alloc(f"xst5_{i}", D) for i in range(2)]
        tmp5 = [A.alloc(f"tmp5_{i}", D) for i in range(2)]
        sm5 = [A.alloc(f"sm5_{i}", 8) for i in range(4)]
        dma('sp', bgluT, b_glu.rearrange("(j p) -> p j", p=128), (), (b_bgluT,), nc_ok=True)
        for c in range(4):
            load_cast(wglu[c][0], wglu[c][1], w_glu[c * 128:(c + 1) * 128, :], 512)
        for c in range(8):
            load_cast(wo[c][0], wo[c][1], w_o[c * 128:(c + 1) * 128, :], 1024)
        psrot['p'] = 0
        for tile_i in range(8):
            pb = next_tps()
            for h in range(4):
                tp(PSH[pb][:, h * 128:(h + 1) * 128], attnO[:, tile_i * 512 + h * 128:tile_i * 512 + (h + 1) * 128], identb,
                   (b_attnO, b_identb), (PSB[pb],))
            cp(ev_eng(), V(mixT, tile_i * 128, [(1024, 4), (1, 128)]), V(PSH[pb], 0, [(128, 4), (1, 128)]), (PSB[pb],), (b_mixT,))
        for blk in range(2):
            for jo in range(4):
                pb = next_ps()
                for ji in range(4):
                    mm(PS[pb], wglu[ji][0][:, jo * 128:(jo + 1) * 128], zT[:, ji * 1024 + blk * 512:ji * 1024 + (blk + 1) * 512],
                       ji == 0, ji == 3, (wglu[ji][1], b_zT), (PSB[pb],))
                sg_t, sg_b = sig[(blk * 4 + jo) % 2]
                act(sg_t, PS[pb], AF.Sigmoid, (PSB[pb], b_bgluT), (sg_b,), bias=bgluT[:, jo:jo + 1])
                tt('dve', mixT[:, (4 + jo) * 1024 + blk * 512:(4 + jo) * 1024 + (blk + 1) * 512],
                   zT[:, jo * 1024 + blk * 512:jo * 1024 + (blk + 1) * 512], sg_t, ALU.mult, (b_zT, sg_b), (b_mixT,))

        def branch_out(tile_i, pbs, gate_idx, xin, xin_b, xout, xout_b, k6):
            cidx = 0 if tile_i < 4 else 1
            sm, sm_b = sm5[k6 % 4]
            t5, t5_b = tmp5[k6 % 2]
            for half in range(2):
                act(t5[:, half * 512:(half + 1) * 512], PS[pbs[half]], AF.Square, (PSB[pbs[half]],), (t5_b, sm_b),
                    accum_out=sm[:, half:half + 1])
            tt('dve', sm[:, 2:3], sm[:, 0:1], sm[:, 1:2], ALU.add, (sm_b,), (sm_b,))
            act(sm[:, 3:4], sm[:, 2:3], AF.Sqrt, (sm_b, b_eps), (sm_b,), bias=epsv[:, 0:1], scale=1.0 / D)
            recip(sm[:, 4:5], sm[:, 3:4], (sm_b,), (sm_b,))
            g0 = (gate_idx * 2 + cidx) * D
            for half in range(2):
                stt('dve', t5[:, half * 512:(half + 1) * 512], PS[pbs[half]], sm[:, 4:5], grow[:, g0 + half * 512:g0 + (half + 1) * 512],
                    ALU.mult, ALU.mult, (PSB[pbs[half]], sm_b, b_grow), (t5_b,))
            tt('pool', xout, t5, xin, ALU.add, (t5_b, xin_b), (xout_b,))

        for tile_i in range(8):
            xs_t, xs_b = xst5[tile_i % 2]
            rows = xp[tile_i * 128:(tile_i + 1) * 128, :] if tile_i < 4 else xs[(tile_i - 4) * 128:(tile_i - 3) * 128, :]
            dma('sp', xs_t, rows, (), (xs_b,))
            pbs = [next_ps(), next_ps()]
            for half in range(2):
                for c in range(8):
                    mm(PS[pbs[half]], mixT[:, c * 1024 + tile_i * 128:c * 1024 + (tile_i + 1) * 128],
                       wo[c][0][:, half * 512:(half + 1) * 512], c == 0, c == 7, (b_mixT, wo[c][1]), (PSB[pbs[half]],))
            branch_out(tile_i, pbs, 0, xs_t, xs_b, x1[tile_i][0], x1[tile_i][1], tile_i)
        if 'x1' in dbg:
            for tile_i in range(8):
                dma('sp', dbg['x1'][tile_i * 128:(tile_i + 1) * 128, :], x1[tile_i][0], (x1[tile_i][1],), (), is_out=True)
        A.release(mark5)
        A.release(A.n, top=True)

        if stop_after == 16:
            S.finalize(nc, sems, dsems, None)
            return nc
        h2T, b_h2T = A.alloc("h2T", 8 * 1024, BF16)
        fT = [A.alloc(f"fT{j}", 1024, BF16) for j in range(22)]
        wfi = [AW.alloc(f"wfi{i}", 2 * 8 * 256, BF16) for i in range(2)]
        xnb7 = [AW.alloc(f"xnb7_{i}", D, BF16) for i in range(2)]
        junk7, b_junk7 = A.alloc("junk7", D, BF16)
        sm7 = [A.alloc(f"sm7_{i}", 8) for i in range(4)]
        sg7 = [A.alloc(f"sg7_{i}", 512, BF16) for i in range(2)]
        tmp7 = [A.alloc(f"tmp7_{i}", D) for i in range(1)]
        for tile_i in range(8):
            cidx = 0 if tile_i < 4 else 1
            xs_t, xs_b = x1[tile_i]
            xn_t, xn_b = xnb7[tile_i % 2]
            sm, sm_b = sm7[tile_i % 4]
            act(junk7, xs_t, AF.Square, (xs_b,), (b_junk7, sm_b), accum_out=sm[:, 0:1])
            act(sm[:, 1:2], sm[:, 0:1], AF.Sqrt, (sm_b, b_eps), (sm_b,), bias=epsv[:, 0:1], scale=1.0 / D)
            recip(sm[:, 2:3], sm[:, 1:2], (sm_b,), (sm_b,))
            ts('dve', xn_t, xs_t, sm[:, 2:3], None, ALU.mult, None, (xs_b, sm_b), (xn_b,))
            pb = next_tps()
            for kc in range(8):
                tp(PSH[pb][:, kc * 128:(kc + 1) * 128], xn_t[:, kc * 128:(kc + 1) * 128], identb, (xn_b, b_identb), (PSB[pb],))
            for kc in range(8):
                ts('dve', h2T[:, kc * 1024 + tile_i * 128:kc * 1024 + (tile_i + 1) * 128], PSH[pb][:, kc * 128:(kc + 1) * 128],
                   sc2[:, kc * 2 + cidx:kc * 2 + cidx + 1], modT[:, 48 + kc * 2 + cidx:48 + kc * 2 + cidx + 1], ALU.mult, ALU.add,
                   (PSB[pb], b_sc2, b_modT), (b_h2T,))
        wfi_src = w_ffn_in.rearrange("(kc p) f -> p kc f", p=128)
        for jp in range(11):
            wt, wb = wfi[jp % 2]
            for gu in range(2):
                for kh in range(2):
                    st_t, st_b = wst['bufs'][wst['n'] % 2]
                    wst['n'] += 1
                    c0 = gu * 2816 + jp * 256
                    dma('sp', V(st_t, 0, [(256, 4), (1, 256)]), wfi_src[:, kh * 4:(kh + 1) * 4, c0:c0 + 256], (), (st_b,))
                    cp('pool', wt[:, (gu * 8 + kh * 4) * 256:(gu * 8 + kh * 4 + 4) * 256], st_t, (st_b,), (wb,))
            for jw in (2 * jp, 2 * jp + 1):
                load_cast(wfo[jw][0], wfo[jw][1], w_ffn_out[jw * 128:(jw + 1) * 128, :], 1024)
            for jj in range(2):
                j = jp * 2 + jj
                for half in range(2):
                    pg, pu = next_ps(), next_ps()
                    for gu, pbank in ((0, pg), (1, pu)):
                        for kc in range(8):
                            mm(PS[pbank], wt[:, (gu * 8 + kc) * 256 + jj * 128:(gu * 8 + kc) * 256 + (jj + 1) * 128],
                               h2T[:, kc * 1024 + half * 512:kc * 1024 + (half + 1) * 512], kc == 0, kc == 7, (wb, b_h2T), (PSB[pbank],))
                    sg_t, sg_b = sg7[(j * 2 + half) % 2]
                    act(sg_t, PS[pg], AF.Silu, (PSB[pg],), (sg_b,))
                    tt('dve', fT[j][0][:, half * 512:(half + 1) * 512], PS[pu], sg_t, ALU.mult, (PSB[pu], sg_b), (fT[j][1],))
        for tile_i in range(8):
            pbs = [next_ps(), next_ps()]
            for half in range(2):
                for j in range(22):
                    mm(PS[pbs[half]], fT[j][0][:, tile_i * 128:(tile_i + 1) * 128], wfo[j][0][:, half * 512:(half + 1) * 512],
                       j == 0, j == 21, (fT[j][1], wfo[j][1]), (PSB[pbs[half]],))
            cidx = 0 if tile_i < 4 else 1
            sm, sm_b = sm7[tile_i % 4]
            t5, t5_b = tmp7[0]
            yo, yo_b = t5, t5_b
            for half in range(2):
                act(junk7[:, half * 512:(half + 1) * 512], PS[pbs[half]], AF.Square, (PSB[pbs[half]],), (b_junk7, sm_b),
                    accum_out=sm[:, half:half + 1])
            tt('dve', sm[:, 2:3], sm[:, 0:1], sm[:, 1:2], ALU.add, (sm_b,), (sm_b,))
            act(sm[:, 3:4], sm[:, 2:3], AF.Sqrt, (sm_b, b_eps), (sm_b,), bias=epsv[:, 0:1], scale=1.0 / D)
            recip(sm[:, 4:5], sm[:, 3:4], (sm_b,), (sm_b,))
            g0 = (1 * 2 + cidx) * D
            for half in range(2):
                stt('dve', t5[:, half * 512:(half + 1) * 512], PS[pbs[half]], sm[:, 4:5], grow[:, g0 + half * 512:g0 + (half + 1) * 512],
                    ALU.mult, ALU.mult, (PSB[pbs[half]], sm_b, b_grow), (t5_b,))
            tt('pool', yo, t5, x1[tile_i][0], ALU.add, (t5_b, x1[tile_i][1]), (yo_b,))
            dst = yp[tile_i * 128:(tile_i + 1) * 128, :] if tile_i < 4 else ys[(tile_i - 4) * 128:(tile_i - 3) * 128, :]
            dma('sp', dst, yo, (yo_b,), (), is_out=True)

        S.finalize(nc, sems, dsems, None)
    return nc


def _rope_fm(tokens):
    t = np.asarray(tokens)
    row = (t // 64).astype(np.float32)
    col = (t % 64).astype(np.float32)
    inv = (np.float32(10000.0) ** (-np.arange(0, 32, 2, dtype=np.float32) / np.float32(32))).astype(np.float32)
    ar = row[:, None] * inv
    ac = col[:, None] * inv
    ang = np.concatenate([ar, ar, ac, ac], -1)
    c = np.cos(ang).astype(np.float32).T
    s = np.sin(ang).astype(np.float32).T
    return np.ascontiguousarray(np.concatenate([c, c], 0)), np.ascontiguousarray(np.concatenate([s, s], 0))


def _consts():
    ident = np.eye(128, dtype=np.float32)
    R = np.zeros((128, 128), np.float32)
    for blk in range(2):
        o = blk * 64
        for f in range(16):
            R[o + f, o + f + 16] = -1.0
            R[o + 16 + f, o + f] = 1.0
            R[o + 32 + f, o + 48 + f] = -1.0
            R[o + 48 + f, o + 32 + f] = 1.0
    tab = np.zeros((128, 64), np.float32)
    tab[:, 0:17] = np.arange(17, dtype=np.float32)[None]
    tab[:, 17:49] = (16.0 * np.arange(32, dtype=np.float32))[None]
    return ident, np.ascontiguousarray(R.T), tab


def prep_core(inputs, j):
    f = lambda a: np.ascontiguousarray(np.asarray(a, dtype=np.float32))
    b, q = j // 4, j % 4
    order = [(q + i) % 4 for i in range(4)]
    xs = f(inputs['x_sample'])[b].reshape(4, 512, 1024)[order].reshape(2048, 1024)
    toks = np.concatenate([np.arange(512) + 512 * sg for sg in order])
    rc, rs = _rope_fm(toks)
    ident, rrotT, tab = _consts()
    flags = np.zeros((128, 32), np.float32)
    for i in range(3):
        sg = (q + 1 + i) % 4
        flags[:, i] = 1.0 if sg < q else 0.0
        flags[:, 3 + i] = 1.0 if sg > q else 0.0
    injF = (3 - q) if q >= 1 else 3
    injB = (2 - q) if q <= 2 else 3
    flags[:, 6 + injF] = 1.0
    flags[:, 10 + injB] = 1.0
    d = {
        'xp': f(inputs['x_prompt'])[2 * j:2 * j + 2].reshape(512, 1024),
        'xs': xs, 'ropec': rc, 'ropes': rs,
        'ck': f(inputs['cache_k'])[b, 0].reshape(256, 512),
        'cv': f(inputs['cache_v'])[b, 0].reshape(256, 512),
        'h0re': f(inputs['state_ssm_re'])[b, 0].reshape(32, 128),
        'h0im': f(inputs['state_ssm_im'])[b, 0].reshape(32, 128),
        'cond2': np.stack([f(inputs['c_ctx']), f(inputs['c'])[b]], 0),
        'w_mod': f(inputs['w_mod'])[0], 'b_mod': f(inputs['b_mod'])[0], 'norm_g': f(inputs['norm_g'])[0],
        'w_in': f(inputs['w_in'])[0], 'lam_params': f(inputs['lam_params'])[0], 'subln_g': f(inputs['subln_g'])[0],
        's_lre': f(inputs['ssm_lambda_re'])[0].reshape(32, 128), 's_lim': f(inputs['ssm_lambda_im'])[0].reshape(32, 128),
        's_lstep': f(inputs['ssm_log_step'])[0].reshape(32, 2),
        's_bre': f(inputs['ssm_b_re'])[0], 's_bim': f(inputs['ssm_b_im'])[0],
        's_cre': f(inputs['ssm_c_re'])[0], 's_cim': f(inputs['ssm_c_im'])[0],
        's_d': f(inputs['ssm_d'])[0], 'w_glu': f(inputs['w_glu'])[0], 'b_glu': f(inputs['b_glu'])[0],
        'w_o': f(inputs['w_o'])[0], 'w_ffn_in': f(inputs['w_ffn_in'])[0], 'w_ffn_out': f(inputs['w_ffn_out'])[0],
        'c_ident': ident, 'c_rrot': rrotT, 'c_tab': tab, 'c_flags': flags,
    }
    return {k: np.ascontiguousarray(v) for k, v in d.items()}


_NC_CACHE = {}


def kernel(**inputs):
    if 'nc' not in _NC_CACHE:
        import os
        _NC_CACHE['nc'] = build_program(stop_after=int(os.environ.get('KSTOP', '99')))
    nc = _NC_CACHE['nc']
    maps = [prep_core(inputs, j) for j in range(8)]
    res = run_bass_kernel_spmd(nc, maps, core_ids=list(range(8))).results
    y_prompt = np.zeros((16, 256, 1024), np.float32)
    y_sample = np.zeros((2, 2048, 1024), np.float32)
    nk = np.zeros((16, 1, 256, 8, 64), np.float32)
    nv = np.zeros((16, 1, 256, 4, 128), np.float32)
    sre = np.zeros((16, 1, 2, 32, 64), np.float32)
    sim_ = np.zeros((16, 1, 2, 32, 64), np.float32)
    for j in range(8):
        r = {k_: (res[j][k_] if k_ in res[j] else 0.0) for k_ in ('yp', 'ys', 'o_nk', 'o_nv', 'o_sre', 'o_sim')}
        b, q = j // 4, j % 4
        if not hasattr(r['yp'], 'shape') or not hasattr(r['o_sre'], 'shape'):
            continue
        y_prompt[2 * j:2 * j + 2] = np.asarray(r['yp']).reshape(2, 256, 1024)
        y_sample[b, q * 512:(q + 1) * 512] = np.asarray(r['ys'])
        nk[2 * j:2 * j + 2, 0] = np.asarray(r['o_nk']).reshape(2, 256, 8, 64)
        nv[2 * j:2 * j + 2, 0] = np.asarray(r['o_nv']).reshape(2, 256, 4, 128)
        sre[2 * j:2 * j + 2, 0] = np.asarray(r['o_sre']).reshape(2, 2, 32, 64)
        sim_[2 * j:2 * j + 2, 0] = np.asarray(r['o_sim']).reshape(2, 2, 32, 64)
    return (y_prompt, y_sample, nk, nv, sre, sim_)
```

```python
import math
import numpy as np
import concourse.bass as bass
import concourse.mybir as mybir
from concourse.bass_utils import run_bass_kernel_spmd

F32 = mybir.dt.float32
BF16 = mybir.dt.bfloat16
AF = mybir.ActivationFunctionType
ALU = mybir.AluOpType
AX = mybir.AxisListType

NS_DMA = 40
ENGS = ['pe', 'act', 'dve', 'pool', 'sp']
SAME_SYNC = True
SAME_DIST = 10 ** 9
T1 = 16
VW = 130
MAGIC = 12582912.0
TWO_PI = 2.0 * math.pi


class Buf:
    __slots__ = ('name', 'w', 'r', 'lo', 'hi', 'excl')

    def __init__(self, name, lo=0, hi=0, excl=False):
        self.name = name
        self.excl = excl
        self.w = None
        self.r = {}
        self.lo = lo
        self.hi = hi


class Sched:
    def __init__(self):
        self.ops = {e: [] for e in ENGS}
        self.ndma = 0
        self.out_dma = []

    def add(self, eng, emit, reads=(), writes=(), dma=False, is_out=False, multi=False):
        ex = [b for b in reads if b.excl]
        if ex:
            reads = [b for b in reads if not b.excl]
            writes = list(writes) + [b for b in ex if b not in writes]
        idx = len(self.ops[eng])
        deps = set()
        for b in reads:
            if b.w is not None:
                deps.add(b.w)
        for b in writes:
            if b.w is not None:
                deps.add(b.w)
            deps.update(b.r.values())
        if dma:
            k = self.ndma
            self.ndma += 1
            key = ('dma', k)
            if k >= NS_DMA:
                deps.add(('dma', k - NS_DMA))
            if is_out:
                self.out_dma.append(key)
        else:
            k = None
            key = (eng, idx)
        self.ops[eng].append(dict(emit=emit, deps=deps, dma=k, inc=False, cnt=0, multi=multi))
        for b in reads:
            if dma:
                b.r[key] = key
            else:
                b.r[eng] = key
        for b in writes:
            b.w = key
            b.r = {}
        return key

    def finalize(self, nc, sems, dsems, engines):
        for eng in ENGS:
            for op_idx, op in enumerate(self.ops[eng]):
                nd = set()
                for d in op['deps']:
                    if d[0] == 'dma':
                        nd.add(d)
                        continue
                    if d[0] == eng and (eng == 'pe' or not SAME_SYNC) and op['dma'] is None:
                        continue
                    if d[0] == eng and op['dma'] is None and (op_idx - d[1]) >= SAME_DIST:
                        continue
                    self.ops[d[0]][d[1]]['inc'] = True
                    nd.add(d)
                op['deps'] = nd
        for eng in ENGS:
            c = 0
            for op in self.ops[eng]:
                if op['inc']:
                    c += 1
                op['cnt'] = c

        def run(eng):
            def body(e):
                known = {}
                for op in self.ops[eng]:
                    need = {}
                    for d in op['deps']:
                        if d[0] == 'dma':
                            slot = d[1] % NS_DMA
                            val = 16 * (d[1] // NS_DMA + 1)
                            sk = ('d', slot)
                        else:
                            val = self.ops[d[0]][d[1]]['cnt']
                            sk = ('e', d[0])
                        if need.get(sk, 0) < val:
                            need[sk] = val
                    todo = []
                    for sk, val in need.items():
                        if known.get(sk, 0) >= val:
                            continue
                        known[sk] = val
                        todo.append((dsems[sk[1]] if sk[0] == 'd' else sems[sk[1]], val))
                    if op['multi']:
                        for (s, val) in todo:
                            e.wait_ge(s, val)
                        todo = []
                    for (s, val) in todo[:-1]:
                        e.wait_ge(s, val)
                    ins = op['emit'](e)
                    if todo:
                        ins._wait_ge(todo[-1][0], todo[-1][1])
                    if op['inc']:
                        ins.then_inc(sems[eng], 1)
                    if op['dma'] is not None:
                        ins.then_inc(dsems[op['dma'] % NS_DMA], 16)
                if eng == 'sp':
                    fin = {}
                    for d in self.out_dma:
                        slot = d[1] % NS_DMA
                        val = 16 * (d[1] // NS_DMA + 1)
                        fin[slot] = max(fin.get(slot, 0), val)
                    for slot, val in fin.items():
                        e.wait_ge(dsems[slot], val)
            return body
        with nc.Block() as block:
            block.tensor(run('pe'))
            block.scalar(run('act'))
            block.vector(run('dve'))
            block.gpsimd(run('pool'))
            block.sync(run('sp'))


class Arena:
    def __init__(self, ap2d, nwords):
        self.base = ap2d
        self.n = nwords
        self.top = 0
        self.ttop = nwords
        self.retired = []
        self.live = []

    def alloc(self, name, nelem, dtype=F32, top=False):
        nb = nelem * (2 if dtype == BF16 else 4)
        nw = (nb + 31) // 32 * 8
        if top:
            hi = self.ttop
            lo = hi - nw
            assert lo >= self.top, f"arena overflow (top) at {name}"
            self.ttop = lo
        else:
            lo = self.top
            hi = lo + nw
            assert hi <= self.ttop, f"arena overflow at {name}: {hi} > {self.ttop}"
            self.top = hi
        v = self.base[:, lo:hi]
        if dtype == BF16:
            v = v.bitcast(BF16)
        v = v[:, 0:nelem]
        b = Buf(name, lo, hi)
        self.live.append(b)
        for (rlo, rhi, keys) in self.retired:
            if rlo < hi and lo < rhi:
                for i, k in enumerate(keys):
                    b.r[('ret', rlo, i)] = k
        return v, b

    def mark(self):
        return self.top

    def release(self, mark, bufs=None, top=False):
        keep = []
        for b in self.live:
            dead = (b.hi <= mark and b.lo >= self.ttop) if top else (b.lo >= mark and b.hi <= self.top)
            if top:
                dead = b.lo >= self.ttop and b.hi <= mark and b.lo >= self.top
            if dead:
                keys = list(b.r.values())
                if b.w is not None:
                    keys.append(b.w)
                if keys:
                    self.retired.append((b.lo, b.hi, keys))
            else:
                keep.append(b)
        self.live = keep
        if top:
            self.ttop = mark
        else:
            self.top = mark


def V(t2d, off, dims):
    return bass.AP(tensor=t2d.tensor, offset=t2d.offset + off,
                   ap=[list(t2d.ap[0])] + [[int(s), int(n)] for (s, n) in dims])


D = 1024
NT_OWN = 8
NTOK_ALL = 2560


def build_program(stop_after=99, debug=(), bisect={}):
    nc = bass.Bass("TRN2", target_bir_lowering=False)
    S = Sched()

    def din(name, shape):
        return nc.dram_tensor(name, list(shape), F32, kind="ExternalInput").ap()

    def dout(name, shape):
        return nc.dram_tensor(name, list(shape), F32, kind="ExternalOutput").ap()

    xp = din("xp", [512, D])
    xs = din("xs", [2048, D])
    ropec = din("ropec", [128, 2048])
    ropes = din("ropes", [128, 2048])
    ck = din("ck", [256, 512])
    cv = din("cv", [256, 512])
    h0re = din("h0re", [32, 128])
    h0im = din("h0im", [32, 128])
    cond2 = din("cond2", [2, D])
    w_mod = din("w_mod", [D, 6 * D])
    b_mod = din("b_mod", [6 * D])
    norm_g = din("norm_g", [4, D])
    w_in = din("w_in", [D, 2048])
    lam_params = din("lam_params", [4, 64])
    subln_g = din("subln_g", [128])
    s_lre = din("s_lre", [32, 128])
    s_lim = din("s_lim", [32, 128])
    s_lstep = din("s_lstep", [32, 2])
    s_bre = din("s_bre", [2, 32, 64, 16])
    s_bim = din("s_bim", [2, 32, 64, 16])
    s_cre = din("s_cre", [2, 32, 16, 64])
    s_cim = din("s_cim", [2, 32, 16, 64])
    s_d = din("s_d", [2, 512])
    w_glu = din("w_glu", [512, 512])
    b_glu = din("b_glu", [512])
    w_o = din("w_o", [D, D])
    w_ffn_in = din("w_ffn_in", [D, 5632])
    w_ffn_out = din("w_ffn_out", [2816, D])
    c_ident = din("c_ident", [128, 128])
    c_rrot = din("c_rrot", [128, 128])
    c_tab = din("c_tab", [128, 64])
    c_flags = din("c_flags", [128, 32])

    yp = dout("yp", [512, D])
    ys = dout("ys", [512, D])
    o_nk = dout("o_nk", [512, 512])
    o_nv = dout("o_nv", [512, 512])
    o_sre = dout("o_sre", [2, 32, 128])
    o_sim = dout("o_sim", [2, 32, 128])
    dbg = {}
    for (nm, shp) in debug:
        dt_ = BF16 if nm.startswith('b_') else F32
        dbg[nm[2:] if nm.startswith('b_') else nm] = nc.dram_tensor(nm, list(shp), dt_, kind="ExternalOutput").ap()

    NW = 52400
    import contextlib
    with contextlib.ExitStack() as es:
        arena_t = es.enter_context(nc.sbuf_tensor("arena", [128, NW], F32))
        psb = [es.enter_context(nc.psum_tensor(f"psb{i}", [128, 512], F32)) for i in range(8)]
        sems = {e: es.enter_context(nc.semaphore(f"s_{e}")) for e in ENGS}
        dsems = [es.enter_context(nc.semaphore(f"d_{i}")) for i in range(NS_DMA)]
        A = Arena(arena_t[:, :], NW)
        PS = [psb[i][:, :] for i in range(8)]
        PSB = [Buf(f"ps{i}", excl=True) for i in range(8)]
        PSH = [PS[i].bitcast(BF16) for i in range(8)]

        def dma(q, out, in_, reads, writes, is_out=False, nc_ok=False):
            def em(e, out=out, in_=in_, nc_ok=nc_ok):
                kw = {}
                if nc_ok:
                    kw['allow_slow_non_contiguous'] = True
                return e.dma_start(out=out, in_=in_, **kw)
            assert q in ('sp', 'act')
            return S.add(q, em, reads, writes, dma=True, is_out=is_out)

        wst = {'n': 0, 'bufs': None}

        def load_cast(dst, dst_b, src, ncols):
            for c0 in range(0, ncols, 1024):
                n = min(1024, ncols - c0)
                st_t, st_b = wst['bufs'][wst['n'] % len(wst['bufs'])]
                wst['n'] += 1
                dma('sp', st_t[:, 0:n], src[:, c0:c0 + n], (), (st_b,))
                cp('pool', dst[:, c0:c0 + n], st_t[:, 0:n], (st_b,), (dst_b,))

        def mm(out, lhsT, rhs, start, stop, reads, writes):
            return S.add('pe', lambda e, o=out, l=lhsT, r=rhs, s=start, t=stop:
                         e.matmul(o, lhsT=l, rhs=r, start=s, stop=t, skip_group_check=True), reads, writes)

        def tp(out, in_, ident, reads, writes):
            return S.add('pe', lambda e, o=out, i=in_, d=ident: e.transpose(o, i, d), reads, writes)

        def act(out, in_, func, reads, writes, bias=None, scale=None, accum_out=None, eng='act'):
            def em(e, out=out, in_=in_, func=func, bias=bias, scale=scale, accum_out=accum_out):
                kw = {}
                if bias is not None:
                    kw['bias'] = bias
                if scale is not None:
                    kw['scale'] = scale
                if accum_out is not None:
                    kw['accum_out'] = accum_out
                return e.activation(out=out, in_=in_, func=func, **kw)
            return S.add(eng, em, reads, writes, multi=(accum_out is not None))

        def ts(eng, out, in0, s1, s2, op0, op1, reads, writes):
            def em(e, out=out, in0=in0, s1=s1, s2=s2, op0=op0, op1=op1):
                if op1 is None:
                    return e.tensor_scalar(out, in0, s1, None, op0)
                return e.tensor_scalar(out, in0, s1, s2, op0, op1)
            return S.add(eng, em, reads, writes)

        def tt(eng, out, in0, in1, op, reads, writes):
            return S.add(eng, lambda e, o=out, a=in0, b=in1, p=op: e.tensor_tensor(o, a, b, p), reads, writes)

        def stt(eng, out, in0, scalar, in1, op0, op1, reads, writes):
            return S.add(eng, lambda e, o=out, a=in0, s=scalar, b=in1, p0=op0, p1=op1:
                         e.scalar_tensor_tensor(o, a, s, b, p0, p1), reads, writes)

        def cp(eng, out, in_, reads, writes):
            if eng == 'act':
                return S.add(eng, lambda e, o=out, i=in_: e.activation(out=o, in_=i, func=AF.Copy), reads, writes)
            return S.add(eng, lambda e, o=out, i=in_: e.tensor_copy(o, i), reads, writes)

        def memset(eng, ap, val, writes):
            return S.add(eng, lambda e, a=ap, v=val: e.memset(a, v), (), writes)

        def recip(out, in_, reads, writes):
            return S.add('dve', lambda e, o=out, i=in_: e.reciprocal(o, i), reads, writes)

        def scan(out, d0, d1, reads, writes):
            return S.add('dve', lambda e, o=out, a=d0, b=d1:
                         e.tensor_tensor_scan(out=o, data0=a, data1=b, initial=0.0, op0=ALU.mult, op1=ALU.add),
                         reads, writes)

        rr = {'n': 0}

        def ev_eng():
            rr['n'] += 1
            return 'act' if rr['n'] % 2 else 'dve'

        psrot = {'t': 0, 'p': 0}

        def next_tps():
            psrot['t'] += 1
            return psrot['t'] % 2

        def next_ps():
            psrot['p'] += 1
            if psrot.get('hold7'):
                return 2 + psrot['p'] % 5
            return 2 + psrot['p'] % 6

        ident, b_ident = A.alloc("ident", 128)
        identb, b_identb = A.alloc("identb", 128, BF16)
        rrot, b_rrot = A.alloc("rrot", 128, BF16)
        ctab, b_ctab = A.alloc("ctab", 64)
        flags, b_flags = A.alloc("flags", 32)
        epsv, b_eps = A.alloc("epsv", 1)
        wst['bufs'] = [A.alloc(f"wst{i}", 1024) for i in range(2)]
        awreg, _b_aw = A.alloc("awreg", 8192)
        A.live.remove(_b_aw)
        AW = Arena(awreg, 8192)
        dma('sp', ident, c_ident[:, :], (), (b_ident,))
        load_cast(identb, b_identb, c_ident, 128)
        load_cast(rrot, b_rrot, c_rrot, 128)
        dma('sp', ctab, c_tab[:, :], (), (b_ctab,))
        dma('sp', flags, c_flags[:, :], (), (b_flags,))
        memset('dve', epsv, 1e-6, (b_eps,))

        winsb = [AW.alloc(f"win{k}", 2048, BF16) for k in range(8)]
        for k in range(8):
            load_cast(winsb[k][0], winsb[k][1], w_in[k * 128:(k + 1) * 128, :], 2048)
        condT, b_condT = A.alloc("condT", 16)
        scondT, b_scondT = A.alloc("scondT", 16, BF16)
        bmodT, b_bmodT = A.alloc("bmodT", 48)
        ngT, b_ngT = A.alloc("ngT", 32)
        modT, b_modT = A.alloc("modT", 96)
        sc1, b_sc1 = A.alloc("sc1", 16)
        sc2, b_sc2 = A.alloc("sc2", 16)
        for c in range(2):
            dma('sp', V(condT, c, [(2, 8)]), cond2[c, :].rearrange("(k p) -> p k", p=128), (), (b_condT,), nc_ok=True)
        dma('sp', bmodT, b_mod.rearrange("(t p) -> p t", p=128), (), (b_bmodT,), nc_ok=True)
        for g in range(4):
            dma('sp', ngT[:, g * 8:(g + 1) * 8], norm_g[g, :].rearrange("(k p) -> p k", p=128), (), (b_ngT,), nc_ok=True)
        act(scondT, condT, AF.Silu, (b_condT,), (b_scondT,))
        scond32, b_scond32 = A.alloc("scond32", 16)
        mark0 = A.mark()
        act(scond32, condT, AF.Silu, (b_condT,), (b_scond32,))
        wm = [A.alloc(f"wm{i}", 2048) for i in range(3)]
        for kc in range(8):
            wt, wb = wm[kc % 3]
            dma('act', wt, w_mod[kc * 128:(kc + 1) * 128, 0:2048], (), (wb,))
            for ct in range(16):
                mm(V(PS[0], ct * 2, [(1, 2)]), wt[:, ct * 128:(ct + 1) * 128], V(scond32, kc * 2, [(1, 2)]),
                   (kc == 0 and ct == 0), (kc == 7), (wb, b_scond32), (PSB[0],))
        tt('dve', V(modT, 0, [(2, 16), (1, 2)]), V(PS[0], 0, [(2, 16), (1, 2)]), V(bmodT, 0, [(1, 16), (0, 2)]),
           ALU.add, (PSB[0], b_bmodT), (b_modT,))

        def mk_sc(sc, bsc, comp, gi):
            ts('dve', sc, V(modT, comp * 16, [(1, 16)]), 1.0, None, ALU.add, None, (b_modT,), (bsc,))
            tt('dve', V(sc, 0, [(2, 8), (1, 2)]), V(sc, 0, [(2, 8), (1, 2)]), V(ngT, gi * 8, [(1, 8), (0, 2)]),
               ALU.mult, (bsc, b_ngT), (bsc,))
        mk_sc(sc1, b_sc1, 1, 0)
        A.release(mark0)


        if stop_after == 10:
            S.finalize(nc, sems, dsems, None)
            return nc
        zT, b_zT = A.alloc("zT", 4 * 1024, BF16, top=True)
        attnO, b_attnO = A.alloc("attnO", 8 * 512, BF16, top=True)
        markB = A.mark()
        uT, b_uT = A.alloc("uT", 4 * NTOK_ALL, BF16)
        uTA, b_uTA = A.alloc("uTA", 4 * 1024, BF16)
        markC = A.mark()
        qT, b_qT = A.alloc("qT", 4 * 1024, BF16)
        kTp, b_kTp = A.alloc("kTp", 4 * 512, BF16)
        kTs, b_kTs = A.alloc("kTs", 4 * 2304, BF16)
        Vp, b_Vp = A.alloc("Vp", 4 * 4 * VW, BF16)
        Vs, b_Vs = A.alloc("Vs", 18 * 4 * VW, BF16)
        wmB = [A.alloc(f"wmB{i}", 2048) for i in range(2)]
        psrot['hold7'] = True

        def modB_dma(p):
            kc, c3 = p // 2, 1 + p % 2
            wt, wb = wmB[p % 2]
            dma('sp', wt, w_mod[kc * 128:(kc + 1) * 128, c3 * 2048:(c3 + 1) * 2048], (), (wb,))

        def modB_mm(p):
            kc, c3 = p // 2, 1 + p % 2
            wt, wb = wmB[p % 2]
            for ct in range(c3 * 16, (c3 + 1) * 16):
                mm(V(PS[7], (ct - 16) * 2, [(1, 2)]), wt[:, (ct - c3 * 16) * 128:(ct - c3 * 16 + 1) * 128], V(scond32, kc * 2, [(1, 2)]),
                   (p == 0 and ct == 16), (kc == 7), (wb, b_scond32), (PSB[7],))

        mark1 = A.mark()
        hTx = [A.alloc(f"hTx{i}", 8 * 512, BF16) for i in range(2)]
        xst = [A.alloc(f"xst{i}", D) for i in range(2)]
        xnb = [A.alloc(f"xnb{i}", D, BF16) for i in range(2)]
        ropet = [A.alloc(f"ropet{i}", 512) for i in range(2)]
        tmpf = [A.alloc(f"tmpf{i}", 512) for i in range(4)]
        rawb = [A.alloc(f"rawb{i}", 512, BF16) for i in range(2)]
        stg = [A.alloc(f"stg{i}", 512) for i in range(2)]
        small = [A.alloc(f"small{i}", 4) for i in range(4)]
        ckb, b_ckb = A.alloc("ckb", 2 * 512, BF16)

        if dbg:
            for (t_, b_) in [(qT, b_qT), (kTp, b_kTp), (kTs, b_kTs), (Vp, b_Vp), (Vs, b_Vs), (uT, b_uT), (uTA, b_uTA)]:
                memset('pool', t_, 0.0, (b_,))
        memset('dve', V(Vp, 128, [(VW, 16)]), 1.0, (b_Vp,))
        memset('dve', V(Vs, 128, [(VW, 72)]), 1.0, (b_Vs,))
        for t in range(2):
            load_cast(ckb[:, t * 512:(t + 1) * 512], b_ckb, ck[t * 128:(t + 1) * 128, :], 512)
            st_t, st_b = wst['bufs'][wst['n'] % 2]
            wst['n'] += 1
            dma('sp', st_t[:, 0:512], cv[t * 128:(t + 1) * 128, :], (), (st_b,))
            cp('pool', V(Vs, t * 4 * VW, [(VW, 4), (1, 128)]), V(st_t, 0, [(128, 4), (1, 128)]), (st_b,), (b_Vs,))
        for t in range(2):
            for h in range(4):
                tp(PSH[0][:, (t * 4 + h) * 128:(t * 4 + h + 1) * 128], ckb[:, t * 512 + h * 128:t * 512 + (h + 1) * 128],
                   identb, (b_ckb, b_identb), (PSB[0],))
        for t in range(2):
            cp('act', V(kTs, t * 128, [(2304, 4), (1, 128)]), V(PSH[0], t * 512, [(128, 4), (1, 128)]), (PSB[0],), (b_kTs,))

        def norm_tile(src_rows, cidx, hdst, hb, tok0, ntok_total, ti):
            xs_t, xs_b = xst[ti % 2]
            xn_t, xn_b = xnb[ti % 2]
            sm, sm_b = small[ti % 4]
            dma('sp', xs_t, src_rows, (), (xs_b,))
            act(xn_t, xs_t, AF.Square, (xs_b,), (xn_b, sm_b), accum_out=sm[:, 0:1])
            act(sm[:, 1:2], sm[:, 0:1], AF.Sqrt, (sm_b, b_eps), (sm_b,), bias=epsv[:, 0:1], scale=1.0 / D)
            recip(sm[:, 2:3], sm[:, 1:2], (sm_b,), (sm_b,))
            ts('dve', xn_t, xs_t, sm[:, 2:3], None, ALU.mult, None, (xs_b, sm_b), (xn_b,))
            pb = next_tps()
            for kc in range(8):
                tp(PSH[pb][:, kc * 128:(kc + 1) * 128], xn_t[:, kc * 128:(kc + 1) * 128], identb, (xn_b, b_identb), (PSB[pb],))
            for kc in range(8):
                ts('dve', hdst[:, kc * ntok_total + tok0:kc * ntok_total + tok0 + 128], PSH[pb][:, kc * 128:(kc + 1) * 128],
                   sc1[:, kc * 2 + cidx:kc * 2 + cidx + 1], modT[:, kc * 2 + cidx:kc * 2 + cidx + 1], ALU.mult, ALU.add,
                   (PSB[pb], b_sc1, b_modT), (hb,))

        def proj_ftile(f, hsrc, hb, ntok_total, tok0):
            pb = next_ps()
            for kc in range(8):
                mm(PS[pb], winsb[kc][0][:, f * 128:(f + 1) * 128], hsrc[:, kc * ntok_total + tok0:kc * ntok_total + tok0 + 512],
                   kc == 0, kc == 7, (winsb[kc][1], hb), (PSB[pb],))
            return pb

        ropecnt = {'n': 0}

        def rope_evac(pb, dst, dst_b, ct, st, cb, sb):
            i = ropecnt['n']
            ropecnt['n'] += 1
            rb, rb_b = rawb[i % 2]
            t1, t1_b = tmpf[(2 * i) % 4]
            t2, t2_b = tmpf[(2 * i + 1) % 4]
            cp('act', rb, PS[pb], (PSB[pb],), (rb_b,))
            pb2 = next_ps()
            mm(PS[pb2], rrot, rb, True, True, (b_rrot, rb_b), (PSB[pb2],))
            tt('dve', t1, PS[pb], ct, ALU.mult, (PSB[pb], cb), (t1_b,))
            tt('dve', t2, PS[pb2], st, ALU.mult, (PSB[pb2], sb), (t2_b,))
            tt('pool', dst, t1, t2, ALU.add, (t1_b, t2_b), (dst_b,))

        blocks = [('P', 0), ('S', 1), ('O', 2), ('O', 3), ('O', 4)]
        if stop_after <= 1:
            blocks = blocks[:bisect.get('nblk', 5)]
        tcount = {'n': 0}

        def blk_vars(kind, bi):
            own = kind != 'O'
            cidx = 0 if kind == 'P' else 1
            hsrc, hb = hTx[bi % 2]
            return own, cidx, hsrc, hb, 512, 0, (0 if kind == 'P' else 512)

        def norm_block(kind, bi):
            own, cidx, hsrc, hb, ntt, t0, q0 = blk_vars(kind, bi)
            ti = tcount['n']
            for t in range(4):
                if kind == 'P':
                    rows = xp[t * 128:(t + 1) * 128, :]
                else:
                    r0 = (bi - 1) * 512 + t * 128
                    rows = xs[r0:r0 + 128, :]
                norm_tile(rows, cidx, hsrc, hb, t0 + t * 128, ntt, ti)
                if 1 <= ti <= 16:
                    modB_mm(ti - 1)
                if ti < 16:
                    modB_dma(ti)
                ti += 1
                tcount['n'] = ti

        def proj_block(kind, bi):
            own, cidx, hsrc, hb, ntt, t0, q0 = blk_vars(kind, bi)
            if bisect.get('noproj'):
                return
            utok0 = bi * 512
            if kind != 'P':
                s0 = (bi - 1) * 512
                ct, cb = ropet[0]
                st, sb = ropet[1]
                dma('sp', ct, ropec[:, s0:s0 + 512], (), (cb,))
                dma('sp', st, ropes[:, s0:s0 + 512], (), (sb,))
            if own and not bisect.get('noqku'):
                for h in range(4):
                    pb = proj_ftile(h, hsrc, hb, ntt, t0)
                    dst = qT[:, h * 1024 + q0:h * 1024 + q0 + 512]
                    if kind == 'P':
                        cp(ev_eng(), dst, PS[pb], (PSB[pb],), (b_qT,))
                    else:
                        rope_evac(pb, dst, b_qT, ct, st, cb, sb)
            for h in range(0 if bisect.get('noqku') else 4):
                pb = proj_ftile(4 + h, hsrc, hb, ntt, t0)
                if kind == 'P':
                    cp(ev_eng(), kTp[:, h * 512:(h + 1) * 512], PS[pb], (PSB[pb],), (b_kTp,))
                else:
                    k0 = 256 + (bi - 1) * 512
                    rope_evac(pb, kTs[:, h * 2304 + k0:h * 2304 + k0 + 512], b_kTs, ct, st, cb, sb)
            for jj in range(0 if bisect.get('noqku') else 4):
                pb = proj_ftile(12 + jj, hsrc, hb, ntt, t0)
                ee = ev_eng()
                cp(ee, V(uT, jj * NTOK_ALL + bi * 32, [(1, 32), (160, 16)]), V(PS[pb], 0, [(16, 32), (1, 16)]), (PSB[pb],), (b_uT,))
                if own:
                    cp(ee, V(uTA, jj * 1024 + bi * 512, [(1, 32), (32, 16)]), V(PS[pb], 0, [(16, 32), (1, 16)]), (PSB[pb],), (b_uTA,))
            for t in range(0 if bisect.get('nov') else 4):
                tok = t0 + t * 128
                pb = next_ps()
                for kc in range(8):
                    mm(PS[pb], hsrc[:, kc * ntt + tok:kc * ntt + tok + 128], winsb[kc][0][:, 1024:1536], kc == 0, kc == 7,
                       (winsb[kc][1], hb), (PSB[pb],))
                if kind == 'P':
                    vdst = V(Vp, t * 4 * VW, [(VW, 4), (1, 128)])
                    vb = b_Vp
                else:
                    vdst = V(Vs, (2 + (bi - 1) * 4 + t) * 4 * VW, [(VW, 4), (1, 128)])
                    vb = b_Vs
                cp('act', vdst, V(PS[pb], 0, [(128, 4), (1, 128)]), (PSB[pb],), (vb,))
                if kind == 'P':
                    sg, sg_b = stg[0]
                    cp('dve', sg, PS[pb], (PSB[pb],), (sg_b,))
                    dma('sp', o_nv[t * 128:(t + 1) * 128, :], sg, (sg_b,), (), is_out=True)
                    pb = next_ps()
                    for kc in range(8):
                        mm(PS[pb], hsrc[:, kc * ntt + tok:kc * ntt + tok + 128], winsb[kc][0][:, 512:1024], kc == 0, kc == 7,
                           (winsb[kc][1], hb), (PSB[pb],))
                    sg, sg_b = stg[1]
                    cp('dve', sg, PS[pb], (PSB[pb],), (sg_b,))
                    dma('sp', o_nk[t * 128:(t + 1) * 128, :], sg, (sg_b,), (), is_out=True)


        norm_block(*blocks[0])
        for ib, (kind, bi) in enumerate(blocks):
            if ib + 1 < len(blocks):
                norm_block(*blocks[ib + 1])
            proj_block(kind, bi)
        if 'qT' in dbg:
            dma('sp', dbg['qT'], qT, (b_qT,), (), is_out=True)
            dma('sp', dbg['kTs'], kTs, (b_kTs,), (), is_out=True)
            dma('sp', dbg['kTp'], kTp, (b_kTp,), (), is_out=True)
            dma('sp', dbg['Vs'], Vs, (b_Vs,), (), is_out=True)
            dma('sp', dbg['Vp'], Vp, (b_Vp,), (), is_out=True)
            dma('sp', dbg['uT'], uT, (b_uT,), (), is_out=True)
        tt('dve', V(modT, 32, [(2, 32), (1, 2)]), V(PS[7], 0, [(2, 32), (1, 2)]), V(bmodT, 16, [(1, 32), (0, 2)]),
           ALU.add, (PSB[7], b_bmodT), (b_modT,))
        mk_sc(sc2, b_sc2, 4, 2)
        psrot['hold7'] = False
        A.release(mark1)
        AW.release(0)


        if stop_after == 12:
            S.finalize(nc, sems, dsems, None)
            return nc
        lamt, b_lamt = A.alloc("lamt", 256 + 16)
        gsub, b_gsub = A.alloc("gsub", 128)
        mark3 = A.mark()
        PTb = [A.alloc(f"PT{i}", 512, BF16) for i in range(8)]
        ctmp = [A.alloc(f"ctmp{i}", 128 + 128 + 8) for i in range(2)]
        dma('sp', lamt[:, 0:256], lam_params.rearrange("a b -> (a b)").partition_broadcast(128), (), (b_lamt,))
        dma('sp', gsub, subln_g.partition_broadcast(128), (), (b_gsub,))
        ts('dve', gsub, gsub, 0.8, None, ALU.mult, None, (b_gsub,), (b_gsub,))
        tt('dve', lamt[:, 0:64], lamt[:, 0:64], lamt[:, 64:128], ALU.mult, (b_lamt,), (b_lamt,))
        tt('dve', lamt[:, 128:192], lamt[:, 128:192], lamt[:, 192:256], ALU.mult, (b_lamt,), (b_lamt,))
        S.add('dve', lambda e: e.reduce_sum(lamt[:, 256:257], lamt[:, 0:64], AX.X), (b_lamt,), (b_lamt,))
        S.add('dve', lambda e: e.reduce_sum(lamt[:, 257:258], lamt[:, 128:192], AX.X), (b_lamt,), (b_lamt,))
        act(lamt[:, 258:260], lamt[:, 256:258], AF.Exp, (b_lamt,), (b_lamt,))
        stt('dve', lamt[:, 260:261], lamt[:, 259:260], -0.2, lamt[:, 258:259], ALU.add, ALU.subtract, (b_lamt,), (b_lamt,))
        neglam = lamt[:, 260:261]

        att = {'s': 0, 'pt': 0, 'c': 0}

        def s_bank():
            att['s'] += 1
            return (0, 1, 6, 7)[att['s'] % 4]

        def combine(tile_idx, h, o0, o1, ob0, ob1):
            cb_t, cb_b = ctmp[att['c'] % 2]
            att['c'] += 1
            tA = cb_t[:, 0:128]
            aT = cb_t[:, 128:256]
            zz = cb_t[:, 256:264]
            recip(zz[:, 0:1], o0[:, 128:129], (ob0,), (cb_b,))
            recip(zz[:, 1:2], o1[:, 128:129], (ob1,), (cb_b,))
            tt('dve', zz[:, 2:3], zz[:, 1:2], neglam, ALU.mult, (cb_b, b_lamt), (cb_b,))
            ts('dve', tA, o0[:, 0:128], zz[:, 0:1], None, ALU.mult, None, (ob0, cb_b), (cb_b,))
            stt('dve', aT, o1[:, 0:128], zz[:, 2:3], tA, ALU.mult, ALU.add, (ob1, cb_b), (cb_b,))
            act(tA, aT, AF.Square, (cb_b,), (cb_b,), accum_out=zz[:, 3:4])
            act(zz[:, 4:5], zz[:, 3:4], AF.Sqrt, (cb_b, b_eps), (cb_b,), bias=epsv[:, 0:1], scale=1.0 / 128)
            recip(zz[:, 5:6], zz[:, 4:5], (cb_b,), (cb_b,))
            stt('dve', attnO[:, tile_idx * 512 + h * 128:tile_idx * 512 + (h + 1) * 128], aT, zz[:, 5:6], gsub,
                ALU.mult, ALU.mult, (cb_b, b_gsub), (b_attnO,))

        OB = [(2, 3), (4, 5)]
        LOOK = 4
        pitems = [(sq, h, m, kb) for sq in range(2) for h in range(4) for m in range(2) for kb in range(2)]
        pst = {}

        def p_score(i):
            sq, h, m, kb = pitems[i]
            sb_ = s_bank()
            mm(PS[sb_][:, 0:256], kTp[64 * m:64 * m + 64, h * 512 + sq * 256 + kb * 128:h * 512 + sq * 256 + (kb + 1) * 128],
               qT[64 * m:64 * m + 64, h * 1024 + sq * 256:h * 1024 + (sq + 1) * 256], True, True, (b_kTp, b_qT), (PSB[sb_],))
            pt_t, pt_b = PTb[att['pt'] % 8]
            att['pt'] += 1
            act(pt_t[:, 0:256], PS[sb_][:, 0:256], AF.Exp, (PSB[sb_],), (pt_b,), scale=0.125)
            pst[i] = (pt_t, pt_b)

        for i in range(min(LOOK, len(pitems))):
            p_score(i)
        for i, (sq, h, m, kb) in enumerate(pitems):
            if i + LOOK < len(pitems):
                p_score(i + LOOK)
            oi_ = (sq * 4 + h) % 2
            ob = OB[m][oi_]
            if kb == 1:
                for qt in range(2):
                    for k2 in range(2):
                        pt_t, pt_b = pst[i - 1 + k2]
                        mm(PS[ob][:, qt * 129:qt * 129 + 129], pt_t[:, qt * 128:(qt + 1) * 128],
                           Vp[:, ((sq * 2 + k2) * 4 + h) * VW:((sq * 2 + k2) * 4 + h) * VW + 129], k2 == 0, k2 == 1,
                           (pt_b, b_Vp), (PSB[ob],))
                if m == 1:
                    for qt in range(2):
                        combine(sq * 2 + qt, h, PS[OB[0][oi_]][:, qt * 129:qt * 129 + 129], PS[OB[1][oi_]][:, qt * 129:qt * 129 + 129],
                                PSB[OB[0][oi_]], PSB[OB[1][oi_]])
        sitems = [(h, m, kb) for h in range(4) for m in range(2) for kb in range(18)]
        sst = {}

        def s_score(i):
            h, m, kb = sitems[i]
            sb_ = s_bank()
            mm(PS[sb_], kTs[64 * m:64 * m + 64, h * 2304 + kb * 128:h * 2304 + (kb + 1) * 128],
               qT[64 * m:64 * m + 64, h * 1024 + 512:h * 1024 + 1024], True, True, (b_kTs, b_qT), (PSB[sb_],))
            pt_t, pt_b = PTb[att['pt'] % 8]
            att['pt'] += 1
            act(pt_t, PS[sb_], AF.Exp, (PSB[sb_],), (pt_b,), scale=0.125)
            sst[i] = (pt_t, pt_b)

        for i in range(LOOK):
            s_score(i)
        for i, (h, m, kb) in enumerate(sitems):
            if i + LOOK < len(sitems):
                s_score(i + LOOK)
            obA, obB = OB[m]
            pt_t, pt_b = sst[i]
            for qt in range(4):
                ob = obA if qt < 3 else obB
                sl = qt if qt < 3 else 0
                first = (kb == 0) and (qt == 0 or qt == 3)
                mm(PS[ob][:, sl * 129:sl * 129 + 129], pt_t[:, qt * 128:(qt + 1) * 128],
                   Vs[:, (kb * 4 + h) * VW:(kb * 4 + h) * VW + 129], first, kb == 17, (pt_b, b_Vs), (PSB[ob],))
            if m == 1 and kb == 17:
                for qt in range(4):
                    sl = qt if qt < 3 else 0
                    i0 = 0 if qt < 3 else 1
                    combine(4 + qt, h, PS[OB[0][i0]][:, sl * 129:sl * 129 + 129], PS[OB[1][i0]][:, sl * 129:sl * 129 + 129],
                            PSB[OB[0][i0]], PSB[OB[1][i0]])
        if 'attnO' in dbg:
            dma('sp', dbg['attnO'], attnO, (b_attnO,), (), is_out=True)
        A.release(mark3)


        if stop_after == 13:
            S.finalize(nc, sems, dsems, None)
            return nc
        A.release(markC)
        if not bisect.get('ssm', True):
            memset('dve', zT, 0.0, (b_zT,))
        else:
            NSL = 32
            g0, b_g0 = A.alloc("g0", 32 * 24)
            def G0(k):
                return g0[:, k * 32:(k + 1) * 32]
            (LR, LI, LST, H0R, H0I, LRC, STEP, XLOG, UTR, TMPA, TMPB, NR, DEN, FRE, FIM, A16R, A16I, RHO, TMPC, TMPD) = range(20)
            rowst, b_rowst = A.alloc("rowst", 5 * 128)
            Bre, b_Bre = A.alloc("Bre", 512)
            Bim, b_Bim = A.alloc("Bim", 512)
            Cre, b_Cre = A.alloc("Cre", 512)
            Cim, b_Cim = A.alloc("Cim", 512)
            BBre, b_BBre = A.alloc("BBre", 512)
            BBim, b_BBim = A.alloc("BBim", 512)
            PWre, b_PWre = A.alloc("PWre", 17 * 32)
            PWim, b_PWim = A.alloc("PWim", 17 * 32)
            MAG, b_MAG = A.alloc("MAG", 17 * 32)
            Rc, b_Rc = A.alloc("Rc", 1024)
            Rs, b_Rs = A.alloc("Rs", 1024)
            dT, b_dT = A.alloc("dT", 12)
            FIN, b_FIN = A.alloc("FIN", 2 * 2 * 32)
            crow = [A.alloc(f"crow{i}", 128) for i in range(1)]
            wk = [AW.alloc("wk0", 1088), AW.alloc("wk1", 1088), AW.alloc("wk2", 512)]
            CAf = [AW.alloc(f"CAf{i}", 17 * 64) for i in range(2)]
            dw = [AW.alloc(f"dw{i}", 160) for i in range(4)]
            for k_, src in enumerate([s_lre, s_lim, None, h0re, h0im]):
                if src is None:
                    dma('sp', crow[0][0][0:32, 0:2], s_lstep[:, :], (), (crow[0][1],))
                    cp('dve', V(rowst, k_ * 128, [(64, 2), (1, 64)])[0:32], V(crow[0][0], 0, [(1, 2), (0, 64)])[0:32],
                       (crow[0][1],), (b_rowst,))
                else:
                    dma('sp', rowst[0:32, k_ * 128:(k_ + 1) * 128], src[:, :], (), (b_rowst,))
            dma('sp', V(Bre, 0, [(16, 32), (1, 16)]), s_bre.rearrange("d (gp g2) p c -> (g2 p) (d gp) c", g2=2), (), (b_Bre,), nc_ok=True)
            dma('sp', V(Bim, 0, [(16, 32), (1, 16)]), s_bim.rearrange("d (gp g2) p c -> (g2 p) (d gp) c", g2=2), (), (b_Bim,), nc_ok=True)
            for d_ in range(2):
                dma('sp', V(dT, d_ * 4, [(1, 4)]), s_d[d_, :].rearrange("(j p) -> p j", p=128), (), (b_dT,), nc_ok=True)
            for k_ in range(5):
                tp(PS[6][:, k_ * 32:(k_ + 1) * 32], rowst[0:32, k_ * 128:(k_ + 1) * 128], ident[0:32, 0:32], (b_rowst, b_ident), (PSB[6],))
            cp('dve', g0[:, 0:160], PS[6][:, 0:160], (PSB[6],), (b_g0,))
            mk_craw = AW.mark()
            craw = AW.alloc("craw", 2048)
            for (Cdst, b_Cdst, csrc, pbC) in ((Cre, b_Cre, s_cre, 7), (Cim, b_Cim, s_cim, 5)):
                cr_t, cr_b = craw
                for g2 in range(2):
                    src = bass.AP(tensor=csrc.tensor, offset=csrc.offset + g2 * 1024, ap=[[2048, 32], [64, 16], [1, 64]])
                    dma('sp', V(cr_t, g2 * 64, [(128, 16), (1, 64)])[0:32], src, (), (cr_b,))
                for c_ in range(16):
                    tp(PS[pbC][:, c_ * 32:(c_ + 1) * 32], cr_t[0:32, c_ * 128:(c_ + 1) * 128], ident[0:32, 0:32],
                       (cr_b, b_ident), (PSB[pbC],))
                cp('dve', V(Cdst, 0, [(1, 16), (16, 32)]), V(PS[pbC], 0, [(32, 16), (1, 32)]), (PSB[pbC],), (b_Cdst,))
            AW.release(mk_craw)
            gb = (b_g0,)
            ts('dve', G0(LRC), G0(LR), -1e-4, None, ALU.min, None, gb, gb)
            act(G0(STEP), G0(LST), AF.Exp, gb, gb)
            tt('dve', G0(XLOG), G0(LRC), G0(STEP), ALU.mult, gb, gb)
            tt('dve', G0(TMPA), G0(LI), G0(STEP), ALU.mult, gb, gb)
            ts('dve', G0(TMPA), G0(TMPA), 1.0 / TWO_PI, None, ALU.mult, None, gb, gb)
            ts('dve', G0(TMPB), G0(TMPA), MAGIC, None, ALU.add, None, gb, gb)
            ts('dve', G0(TMPB), G0(TMPB), -MAGIC, None, ALU.add, None, gb, gb)
            tt('dve', G0(UTR), G0(TMPA), G0(TMPB), ALU.subtract, gb, gb)

            def sincos(dst_c, dst_s, bc, bs, yv, n, wka, wkb):
                ra, rab = wka
                rb, rbb = wkb
                ts('dve', ra[:, 0:n], yv, MAGIC, None, ALU.add, None, (wk[0][1],), (rab,))
                ts('dve', ra[:, 0:n], ra[:, 0:n], -MAGIC, None, ALU.add, None, (rab,), (rab,))
                tt('dve', ra[:, 0:n], yv, ra[:, 0:n], ALU.subtract, (wk[0][1], rab), (rab,))
                act(dst_s, ra[:, 0:n], AF.Sin, (rab,), (bs,), scale=TWO_PI)
                ts('dve', rb[:, 0:n], yv, 0.25, None, ALU.add, None, (wk[0][1],), (rbb,))
                ts('dve', ra[:, 0:n], rb[:, 0:n], MAGIC, None, ALU.add, None, (rbb,), (rab,))
                ts('dve', ra[:, 0:n], ra[:, 0:n], -MAGIC, None, ALU.add, None, (rab,), (rab,))
                tt('dve', ra[:, 0:n], rb[:, 0:n], ra[:, 0:n], ALU.subtract, (rbb, rab), (rab,))
                act(dst_c, ra[:, 0:n], AF.Sin, (rab,), (bc,), scale=TWO_PI)

            mt17 = V(ctab, 0, [(1, 17), (0, 32)])
            tt('dve', V(MAG, 0, [(32, 17), (1, 32)]), V(g0, XLOG * 32, [(0, 17), (1, 32)]), mt17, ALU.mult, (b_g0, b_ctab), (b_MAG,))
            act(MAG, MAG, AF.Exp, (b_MAG,), (b_MAG,))
            tt('dve', V(wk[0][0], 0, [(32, 17), (1, 32)]), V(g0, UTR * 32, [(0, 17), (1, 32)]), mt17, ALU.mult, (b_g0, b_ctab), (wk[0][1],))
            sincos(PWre, PWim, b_PWre, b_PWim, wk[0][0][:, 0:544], 544, CAf[0], CAf[1])
            tt('dve', PWre, PWre, MAG, ALU.mult, (b_PWre, b_MAG), (b_PWre,))
            tt('dve', PWim, PWim, MAG, ALU.mult, (b_PWim, b_MAG), (b_PWim,))
            tt('dve', V(wk[0][0], 0, [(32, 32), (1, 32)]), V(g0, UTR * 32, [(1, 32), (0, 32)]), V(ctab, 17, [(0, 32), (1, 32)]),
               ALU.mult, (b_g0, b_ctab), (wk[0][1],))
            sincos(Rc, Rs, b_Rc, b_Rs, wk[0][0][:, 0:1024], 1024, CAf[0], CAf[1])
            P1r = PWre[:, 32:64]
            P1i = PWim[:, 32:64]
            ts('dve', G0(NR), P1r, -1.0, None, ALU.add, None, (b_PWre,), gb)
            tt('dve', G0(DEN), G0(LRC), G0(LRC), ALU.mult, gb, gb)
            tt('dve', G0(TMPA), G0(LI), G0(LI), ALU.mult, gb, gb)
            tt('dve', G0(DEN), G0(DEN), G0(TMPA), ALU.add, gb, gb)
            recip(G0(DEN), G0(DEN), gb, gb)
            tt('dve', G0(TMPA), G0(NR), G0(LRC), ALU.mult, gb, gb)
            tt('dve', G0(TMPB), P1i, G0(LI), ALU.mult, (b_PWim, b_g0), gb)
            tt('dve', G0(TMPA), G0(TMPA), G0(TMPB), ALU.add, gb, gb)
            tt('dve', G0(FRE), G0(TMPA), G0(DEN), ALU.mult, gb, gb)
            tt('dve', G0(TMPA), P1i, G0(LRC), ALU.mult, (b_PWim, b_g0), gb)
            tt('dve', G0(TMPB), G0(NR), G0(LI), ALU.mult, gb, gb)
            tt('dve', G0(TMPA), G0(TMPA), G0(TMPB), ALU.subtract, gb, gb)
            tt('dve', G0(FIM), G0(TMPA), G0(DEN), ALU.mult, gb, gb)
            def bc16(k):
                return V(g0, k * 32, [(1, 32), (0, 16)])
            B3 = lambda t: V(t, 0, [(16, 32), (1, 16)])
            w0, w0b = wk[0]
            w1, w1b = wk[1]
            tt('dve', B3(w0), bc16(FRE), B3(Bre), ALU.mult, (b_g0, b_Bre), (w0b,))
            tt('dve', B3(w1), bc16(FIM), B3(Bim), ALU.mult, (b_g0, b_Bim), (w1b,))
            tt('dve', B3(BBre), B3(w0), B3(w1), ALU.subtract, (w0b, w1b), (b_BBre,))
            tt('dve', B3(w0), bc16(FRE), B3(Bim), ALU.mult, (b_g0, b_Bim), (w0b,))
            tt('dve', B3(w1), bc16(FIM), B3(Bre), ALU.mult, (b_g0, b_Bre), (w1b,))
            tt('dve', B3(BBim), B3(w0), B3(w1), ALU.add, (w0b, w1b), (b_BBim,))
            cp('dve', G0(A16R), PWre[:, 16 * 32:17 * 32], (b_PWre,), gb)
            cp('dve', G0(A16I), PWim[:, 16 * 32:17 * 32], (b_PWim,), gb)
            cp('dve', G0(RHO), MAG[:, 16 * 32:17 * 32], (b_MAG,), gb)
            tt('dve', dT[:, 8:12], dT[:, 0:4], dT[:, 4:8], ALU.add, (b_dT,), (b_dT,))
            memset('dve', FIN, 0.0, (b_FIN,))
            AHE, b_AHE = A.alloc("AHE", 4 * 32)
            def cmul32(dre, dim, ar, ai, br, bi, rd):
                tt('dve', G0(TMPA), ar, br, ALU.mult, rd, gb)
                tt('dve', G0(TMPB), ai, bi, ALU.mult, rd, gb)
                tt('dve', dre, G0(TMPA), G0(TMPB), ALU.subtract, gb, (b_AHE,))
                tt('dve', G0(TMPA), ar, bi, ALU.mult, rd, gb)
                tt('dve', G0(TMPB), ai, br, ALU.mult, rd, gb)
                tt('dve', dim, G0(TMPA), G0(TMPB), ALU.add, gb, (b_AHE,))
            cmul32(AHE[:, 0:32], AHE[:, 32:64], G0(A16R), G0(A16I), G0(H0R), G0(H0I), (b_g0,))
            cmul32(AHE[:, 64:96], AHE[:, 96:128], G0(A16R), G0(A16I), V(Rc, 31, [(32, 32)]), V(Rs, 31, [(32, 32)]), (b_g0, b_Rc, b_Rs))

            Yst = [A.alloc(f"Yst{i}", 4 * 2 * 2 * 128, BF16) for i in range(2)]
            BT, b_BT = A.alloc("BT", 16 * 2 * 2 * 128, BF16)
            RHl, b_RHl = AW.alloc("RHl", 4 * 2 * 512, BF16)
            RHcs = [A.alloc(f"RHc{i}", 4 * 2 * 512, BF16) for i in range(2)]
            BBblk, b_BBblk = A.alloc("BBblk", 4 * 2 * 128, BF16)
            Kbds = [A.alloc(f"Kbd{i}", 33 * 128, BF16) for i in range(2)]
            Gt, b_Gt = A.alloc("Gt", 2 * 5 * 128)
            GSt, b_GSt = A.alloc("GSt", 2 * 5 * 128)
            TC, b_TC = A.alloc("TC", 5 * 128)
            TS, b_TS = A.alloc("TS", 5 * 128)
            D0, b_D0 = AW.alloc("D0", 2 * 128)
            Hbuf, b_Hbuf = A.alloc("Hbuf", 2 * 2 * 4 * 64, BF16)
            Hr, b_Hr = AW.alloc("Hr", 2 * 128)
            sml, b_sml = AW.alloc("sml", 64)
            Ysb, b_Ysb = A.alloc("Ysb", 2 * 4 * 512, BF16)
            Dsk, b_Dsk = A.alloc("Dsk", 128, BF16)
            for t_, b_ in ((Yst[0][0], Yst[0][1]), (Yst[1][0], Yst[1][1]), (RHl, b_RHl), (RHcs[0][0], RHcs[0][1]), (RHcs[1][0], RHcs[1][1]), (BBblk, b_BBblk), (Hbuf, b_Hbuf)):
                memset('pool', t_, 0.0, (b_,))
            ssm_ps = {'n': 0}

            def sps():
                ssm_ps['n'] += 1
                return 2 + ssm_ps['n'] % 4

            def gen_unit(j, dr):
                s0 = dr * 16 + 4 * j
                u = j * 2 + dr
                RHc, b_RHc = RHcs[u % 2]
                Kbd, b_Kbd = Kbds[j % 2]
                (w0, w0b), (w1, w1b) = wk[0], wk[1]
                first = 0 if dr == 0 else 31
                pw_r = V(PWre, s0, [(32, 17), (1, 4), (0, 16)])
                pw_i = V(PWim, s0, [(32, 17), (1, 4), (0, 16)])
                c_r = V(Cre, s0 * 16, [(0, 17), (16, 4), (1, 16)])
                c_i = V(Cim, s0 * 16, [(0, 17), (16, 4), (1, 16)])
                W3 = lambda t: V(t, 0, [(64, 17), (16, 4), (1, 16)])
                tt('pool', W3(w0), pw_r, c_r, ALU.mult, (b_PWre, b_Cre), (w0b,))
                tt('pool', W3(w1), pw_i, c_i, ALU.mult, (b_PWim, b_Cim), (w1b,))
                tt('pool', W3(CAf[0][0]), W3(w0), W3(w1), ALU.subtract, (w0b, w1b), (CAf[0][1],))
                tt('pool', W3(w0), pw_i, c_r, ALU.mult, (b_PWim, b_Cre), (w0b,))
                tt('pool', W3(w1), pw_r, c_i, ALU.mult, (b_PWre, b_Cim), (w1b,))
                tt('pool', W3(w0), W3(w0), W3(w1), ALU.add, (w0b, w1b), (w0b,))
                ts('pool', W3(CAf[1][0]), W3(w0), -1.0, None, ALU.mult, None, (w0b,), (CAf[1][1],))
                for ri, BBt, bBB in ((0, BBre, b_BBre), (1, BBim, b_BBim)):
                    for g2 in range(2):
                        cp('act', V(BBblk, ri * 128 + g2 * 16, [(256 + 32, 4), (1, 16)])[g2 * 64:(g2 + 1) * 64],
                           V(BBt, s0 * 16, [(16, 4), (1, 16)])[g2 * 64:(g2 + 1) * 64], (bBB,), (b_BBblk,))
                for ri in range(2):
                    for g2 in range(2):
                        cp('act', V(RHl, ri * 512 + g2 * 16, [(1024, 4), (32, 16), (1, 16)])[g2 * 64:(g2 + 1) * 64],
                           V(CAf[ri][0], 0, [(16, 4), (64, 16), (1, 16)])[g2 * 64:(g2 + 1) * 64], (CAf[ri][1],), (b_RHl,))
                for r in range(4):
                    pb = sps()
                    for ri in range(2):
                        mm(PS[pb], BBblk[:, (r * 2 + ri) * 128:(r * 2 + ri + 1) * 128], RHl[:, (r * 2 + ri) * 512:(r * 2 + ri + 1) * 512],
                           ri == 0, ri == 1, (b_BBblk, b_RHl), (PSB[pb],))
                    cp('act', V(Kbd, dr * 16 * 128 + r * 32, [(128, 16), (1, 32)]), V(PS[pb], 0, [(32, 16), (1, 32)]),
                       (PSB[pb],), (b_Kbd,))
                for ri in range(2):
                    for g2 in range(2):
                        if dr == 0:
                            src = V(CAf[ri][0], 64, [(16, 4), (64, 16), (1, 16)])
                        else:
                            src = V(CAf[ri][0], 16 * 64, [(16, 4), (-64, 16), (1, 16)])
                        cp('act', V(RHc, ri * 512 + g2 * 16, [(1024, 4), (32, 16), (1, 16)])[g2 * 64:(g2 + 1) * 64],
                           src[g2 * 64:(g2 + 1) * 64], (CAf[ri][1],), (b_RHc,))
                pend_ev = []
                for th in range(4):
                    ys_t, ys_b = Yst[th % 2]
                    if dr == 0:
                        e0, es = 15 - th * 4, -32
                    else:
                        e0, es = th * 4, 32
                    pwr = V(PWre, e0 * 32 + s0, [(es, 4), (1, 4), (0, 16)])
                    pwi = V(PWim, e0 * 32 + s0, [(es, 4), (1, 4), (0, 16)])
                    bbr = V(BBre, s0 * 16, [(0, 4), (16, 4), (1, 16)])
                    bbi = V(BBim, s0 * 16, [(0, 4), (16, 4), (1, 16)])
                    X3 = lambda t: V(t, 0, [(64, 4), (16, 4), (1, 16)])
                    w2, w2b = wk[2]
                    for ri in range(2):
                        if ri == 0:
                            tt('pool', X3(w0), pwr, bbr, ALU.mult, (b_PWre, b_BBre), (w0b,))
                            tt('pool', X3(w1), pwi, bbi, ALU.mult, (b_PWim, b_BBim), (w1b,))
                            tt('pool', X3(w2), X3(w0), X3(w1), ALU.subtract, (w0b, w1b), (w2b,))
                        else:
                            tt('pool', X3(w0), pwr, bbi, ALU.mult, (b_PWre, b_BBim), (w0b,))
                            tt('pool', X3(w1), pwi, bbr, ALU.mult, (b_PWim, b_BBre), (w1b,))
                            tt('pool', X3(w2), X3(w0), X3(w1), ALU.add, (w0b, w1b), (w2b,))
                        for par in range(2):
                            for g2 in range(2):
                                cp('act', V(ys_t, ri * 256 + par * 128 + par * 32 + g2 * 16, [(512, 4), (64, 2), (1, 16)])[g2 * 64:(g2 + 1) * 64],
                                   V(w2, par * 16, [(64, 4), (32, 2), (1, 16)])[g2 * 64:(g2 + 1) * 64], (w2b,), (ys_b,))
                    for (bt_sl, pbk) in pend_ev:
                        cp('act', bt_sl, PSH[pbk], (PSB[pbk],), (b_BT,))
                    pend_ev.clear()
                    pbt = next_tps()
                    for t8 in range(4):
                        for ri in range(2):
                            for par in range(2):
                                col = ((t8 % 2) * 4 + ri * 2 + par) * 128
                                tp(PSH[pbt][:, col:col + 128], ys_t[:, (t8 * 4 + ri * 2 + par) * 128:(t8 * 4 + ri * 2 + par + 1) * 128],
                                   identb, (ys_b, b_identb), (PSB[pbt],))
                        if t8 % 2 == 1:
                            tau0 = th * 4 + t8 - 1
                            pend_ev.append((BT[:, tau0 * 512:(tau0 + 2) * 512], pbt))
                            pbt = next_tps()
                for (bt_sl, pbk) in pend_ev:
                    cp('act', bt_sl, PSH[pbk], (PSB[pbk],), (b_BT,))
                pend_ev.clear()

            def head_unit(j, dr):
                s0 = dr * 16 + 4 * j
                u = j * 2 + dr
                RHc, b_RHc = RHcs[u % 2]
                Kbd, b_Kbd = Kbds[j % 2]
                (w0, w0b), (w1, w1b) = wk[0], wk[1]
                first = 0 if dr == 0 else 31
                for sl in range(5):
                    if sl == 0:
                        if dr == 0:
                            srcc = V(Rc, s0 * 32, [(32, 4), (0, 2), (1, 16)])
                            srcs = V(Rs, s0 * 32, [(32, 4), (0, 2), (1, 16)])
                        else:
                            srcc = V(Rc, s0 * 32 + 15, [(32, 4), (0, 2), (-1, 16)])
                            srcs = V(Rs, s0 * 32 + 15, [(32, 4), (0, 2), (-1, 16)])
                        dstc = V(TC, sl * 128, [(32, 4), (16, 2), (1, 16)])
                        dsts = V(TS, sl * 128, [(32, 4), (16, 2), (1, 16)])
                    else:
                        if dr == 0:
                            srcc = V(Rc, s0 * 32, [(32, 4), (1, 32)])
                            srcs = V(Rs, s0 * 32, [(32, 4), (1, 32)])
                        else:
                            srcc = V(Rc, s0 * 32 + 31, [(32, 4), (-1, 32)])
                            srcs = V(Rs, s0 * 32 + 31, [(32, 4), (-1, 32)])
                        dstc = V(TC, sl * 128, [(32, 4), (1, 32)])
                        dsts = V(TS, sl * 128, [(32, 4), (1, 32)])
                    if sl >= 2:
                        fl = flags[:, dr * 3 + (sl - 2):dr * 3 + (sl - 2) + 1]
                        ts('dve', dstc, srcc, fl, None, ALU.mult, None, (b_Rc, b_flags), (b_TC,))
                        ts('dve', dsts, srcs, fl, None, ALU.mult, None, (b_Rs, b_flags), (b_TS,))
                    else:
                        cp('dve', dstc, srcc, (b_Rc,), (b_TC,))
                        cp('dve', dsts, srcs, (b_Rs,), (b_TS,))
                for ty in range(2):
                    cp('dve', V(D0, ty * 128, [(32, 4), (1, 32)]), V(g0, RHO * 32 + s0, [(1, 4), (0, 32)]), (b_g0,), (b_D0,))
                first = 0 if dr == 0 else 31
                memset('dve', V(D0, 128 + first, [(32, 4)]), 0.0, (b_D0,))
                memset('dve', V(D0, first if dr == 0 else 15, [(16, 8)]), 0.0, (b_D0,))
                for r in range(4):
                    pb = sps()
                    hp = 64 * (r // 2)
                    for ri in range(2):
                        for tau in range(16):
                            mm(PS[pb][:, ri * 160:(ri + 1) * 160],
                               BT[hp:hp + 64, (tau * 4 + ri * 2 + (r % 2)) * 128:(tau * 4 + ri * 2 + (r % 2) + 1) * 128],
                               uT[hp:hp + 64, j * NTOK_ALL + tau * 160:j * NTOK_ALL + (tau + 1) * 160], tau == 0, tau == 15, (b_BT, b_uT), (PSB[pb],))
                    s_re = V(PS[pb], 0, [(32, 5), (1, 32)])
                    s_im = V(PS[pb], 160, [(32, 5), (1, 32)])
                    tc_ = V(TC, r * 32, [(128, 5), (1, 32)])
                    ts_ = V(TS, r * 32, [(128, 5), (1, 32)])
                    g_re = V(Gt, r * 32, [(128, 5), (1, 32)])
                    g_im = V(Gt, 640 + r * 32, [(128, 5), (1, 32)])
                    S5v = lambda t: V(t, 0, [(32, 5), (1, 32)])
                    (w0, w0b), (w1, w1b), (w2_, w2b_), (w3_, w3b_) = dw
                    tt('dve', S5v(w0), s_re, tc_, ALU.mult, (PSB[pb], b_TC), (w0b,))
                    tt('dve', S5v(w1), s_im, ts_, ALU.mult, (PSB[pb], b_TS), (w1b,))
                    tt('dve', S5v(w2_), s_im, tc_, ALU.mult, (PSB[pb], b_TC), (w2b_,))
                    tt('dve', S5v(w3_), s_re, ts_, ALU.mult, (PSB[pb], b_TS), (w3b_,))
                    tt('dve', g_re, S5v(w0), S5v(w1), ALU.add, (w0b, w1b), (b_Gt,))
                    tt('dve', g_im, S5v(w2_), S5v(w3_), ALU.subtract, (w2b_, w3b_), (b_Gt,))

            def tail_unit(j, dr):
                s0 = dr * 16 + 4 * j
                u = j * 2 + dr
                RHc, b_RHc = RHcs[u % 2]
                Kbd, b_Kbd = Kbds[j % 2]
                (w0, w0b), (w1, w1b) = wk[0], wk[1]
                first = 0 if dr == 0 else 31
                (w0, w0b), (w1, w1b) = wk[0], wk[1]
                first = 0 if dr == 0 else 31
                sm_ = lambda a: sml[:, a:a + 4]
                a16r = g0[:, A16R * 32 + s0:A16R * 32 + s0 + 4]
                a16i = g0[:, A16I * 32 + s0:A16I * 32 + s0 + 4]
                h0r_ = g0[:, H0R * 32 + s0:H0R * 32 + s0 + 4]
                h0i_ = g0[:, H0I * 32 + s0:H0I * 32 + s0 + 4]
                sb_ = (b_sml,)

                def cmul(dre, dim, ar, ai, br, bi, rd):
                    tt('dve', sm_(8), ar, br, ALU.mult, rd, sb_)
                    tt('dve', sm_(40), ar, bi, ALU.mult, rd, sb_)
                    tt('dve', sm_(12), ai, bi, ALU.mult, rd, sb_)
                    tt('dve', sm_(44), ai, br, ALU.mult, rd, sb_)
                    tt('dve', dre, sm_(8), sm_(12), ALU.subtract, sb_, sb_)
                    tt('dve', dim, sm_(40), sm_(44), ALU.add, sb_, sb_)
                lastpos = 31 if dr == 0 else 0
                rc31 = V(Rc, s0 * 32 + 31, [(32, 4)])
                rs31 = V(Rs, s0 * 32 + 31, [(32, 4)])
                order = [2, 3, 4, 1] if dr == 0 else [4, 3, 2, 1]
                inj_idx = {2: 0, 3: 1, 4: 2, 1: 3}
                d0P = D0[:, 0:128]
                d0S = D0[:, 128:256]

                def gslot(t, ri, sl):
                    return t[:, (ri * 5 + sl) * 128:(ri * 5 + sl + 1) * 128]

                def do_scan(sl):
                    d0 = d0P if sl == 0 else d0S
                    for ri in range(2):
                        if dr == 0:
                            scan(gslot(GSt, ri, sl), d0, gslot(Gt, ri, sl), (b_D0, b_Gt), (b_GSt,))
                        else:
                            scan(gslot(GSt, ri, sl)[:, ::-1], d0[:, ::-1], gslot(Gt, ri, sl)[:, ::-1], (b_D0, b_Gt), (b_GSt,))
                do_scan(0)
                for oi, sl in enumerate(order):
                    fl = flags[:, 6 + dr * 4 + inj_idx[sl]:6 + dr * 4 + inj_idx[sl] + 1]
                    for ri in range(2):
                        gf = V(Gt, (ri * 5 + sl) * 128 + first, [(32, 4)])
                        stt('dve', gf, AHE[:, ri * 32 + s0:ri * 32 + s0 + 4], fl, gf, ALU.mult, ALU.add, (b_AHE, b_flags, b_Gt), (b_Gt,))
                    if oi > 0:
                        psl = order[oi - 1]
                        glr = V(GSt, (0 * 5 + psl) * 128 + lastpos, [(32, 4)])
                        gli = V(GSt, (1 * 5 + psl) * 128 + lastpos, [(32, 4)])
                        if sl == 1:
                            cmul(sm_(24), sm_(28), rc31, rs31, glr, gli, (b_Rc, b_Rs, b_GSt, b_sml))
                            cmul(sm_(32), sm_(36), a16r, a16i, sm_(24), sm_(28), (b_g0, b_sml))
                        else:
                            cmul(sm_(32), sm_(36), AHE[:, 64 + s0:64 + s0 + 4], AHE[:, 96 + s0:96 + s0 + 4], glr, gli, (b_GSt, b_sml, b_AHE))
                        for ri in range(2):
                            gf = V(Gt, (ri * 5 + sl) * 128 + first, [(32, 4)])
                            tt('dve', gf, gf, sm_(32 + 4 * ri), ALU.add, (b_Gt, b_sml), (b_Gt,))
                    do_scan(sl)
                flo = flags[:, 6 + dr * 4 + 3:6 + dr * 4 + 4]
                stt('dve', sm_(24), h0r_, flo, sm_(24), ALU.mult, ALU.add, (b_g0, b_flags, b_sml), sb_)
                stt('dve', sm_(28), h0i_, flo, sm_(28), ALU.mult, ALU.add, (b_g0, b_flags, b_sml), sb_)
                hb0 = dr * (2 * 4 * 64)
                (w0, w0b), (w1, w1b) = dw[0], dw[1]
                for sl in (0, 1):
                    tcs = TC[:, sl * 128:(sl + 1) * 128]
                    tss = TS[:, sl * 128:(sl + 1) * 128]
                    gr = gslot(GSt, 0, sl)
                    gi = gslot(GSt, 1, sl)
                    for ri in range(2):
                        if ri == 0:
                            tt('dve', w0[:, 0:128], tcs, gr, ALU.mult, (b_TC, b_GSt), (w0b,))
                            tt('dve', w1[:, 0:128], tss, gi, ALU.mult, (b_TS, b_GSt), (w1b,))
                            tt('dve', Hr[:, 0:128], w0[:, 0:128], w1[:, 0:128], ALU.subtract, (w0b, w1b), (b_Hr,))
                        else:
                            tt('dve', w0[:, 0:128], tss, gr, ALU.mult, (b_TS, b_GSt), (w0b,))
                            tt('dve', w1[:, 0:128], tcs, gi, ALU.mult, (b_TC, b_GSt), (w1b,))
                            tt('dve', Hr[:, 0:128], w0[:, 0:128], w1[:, 0:128], ALU.add, (w0b, w1b), (b_Hr,))
                        hoff = hb0 + ri * (4 * 64)
                        sh = 1 if dr == 0 else 0
                        if sl == 0:
                            cp('dve', V(Hbuf, hoff + sh, [(64, 4), (16, 2), (1, 15)]), V(Hr, 1 - sh, [(32, 4), (16, 2), (1, 15)]), (b_Hr,), (b_Hbuf,))
                            fpos = 15 if dr == 0 else 0
                            cp('dve', V(FIN, ri * 64 + s0, [(32, 2), (1, 4)]), V(Hr, fpos, [(16, 2), (32, 4)]), (b_Hr,), (b_FIN,))
                        else:
                            cp('dve', V(Hbuf, hoff + 32 + sh, [(64, 4), (1, 31)]), V(Hr, 1 - sh, [(32, 4), (1, 31)]), (b_Hr,), (b_Hbuf,))
                            ipos = 32 if dr == 0 else 63
                            cp('dve', V(Hbuf, hoff + ipos, [(64, 4)]), sm_(24 + 4 * ri), (b_sml,), (b_Hbuf,))
                (w0, w0b), (w1, w1b) = wk[0], wk[1]
                first = 0 if dr == 0 else 31
                for blk in range(2):
                    for r in range(4):
                        pb = sps()
                        for ri in range(2):
                            hoff = hb0 + ri * (4 * 64) + r * 64 + blk * 32
                            lh = Hbuf[:, hoff:hoff + 32]
                            mm(PS[pb][0:32, :], lh, RHc[:, (r * 2 + ri) * 512:(r * 2 + ri + 1) * 512], ri == 0, ri == 1,
                               (b_Hbuf, b_RHc), (PSB[pb],))
                        yv = V(Ysb, blk * 2048 + r * 32, [(128, 16), (1, 32)])[0:32]
                        if dr == 0:
                            cp('act', yv, V(PS[pb], 0, [(32, 16), (1, 32)])[0:32], (PSB[pb],), (b_Ysb,))
                        else:
                            tt('dve', yv, V(PS[pb], 0, [(32, 16), (1, 32)])[0:32], yv, ALU.add, (PSB[pb], b_Ysb), (b_Ysb,))

            def yacc(j):
                Kbd, b_Kbd = Kbds[j % 2]
                for blk in range(2):
                    pby = 6 + blk
                    ub = j * 1024 + blk * 512
                    mm(PS[pby], Dsk, uTA[:, ub:ub + 512], True, False, (b_Dsk, b_uTA), (PSB[pby],))
                    for l in range(16):
                        n_ = (16 - l) * 32
                        mm(PS[pby][:, l * 32:512], Kbd[:, l * 128:(l + 1) * 128], uTA[:, ub:ub + n_],
                           False, False, (b_Kbd, b_uTA), (PSB[pby],))
                        mm(PS[pby][:, 0:n_], Kbd[:, (16 + l) * 128:(17 + l) * 128], uTA[:, ub + l * 32:ub + 512],
                           False, False, (b_Kbd, b_uTA), (PSB[pby],))
                    for tau in range(16):
                        mm(PS[pby][:, tau * 32:(tau + 1) * 32], Ysb[0:32, blk * 2048 + tau * 128:blk * 2048 + (tau + 1) * 128], identb[0:32, 0:32],
                           False, tau == 15, (b_Ysb, b_identb), (PSB[pby],))
                    act(V(zT, j * 1024 + blk * 512, [(1, 16), (16, 32)]), V(PS[pby], 0, [(32, 16), (1, 32)]), AF.Gelu_apprx_tanh, (PSB[pby],), (b_zT,))
                    if 'yssm' in dbg:
                        cp('dve', V(wk[0][0], 0, [(1, 16), (16, 32)]), V(PS[pby], 0, [(32, 16), (1, 32)]), (PSB[pby],), (wk[0][1],))
                        dma('sp', dbg['yssm'][:, j * 1024 + blk * 512:j * 1024 + (blk + 1) * 512], wk[0][0][:, 0:512], (wk[0][1],), (), is_out=True)

            gen_unit(0, 0)
            for j in range(4):
                ts('dve', Dsk, identb, dT[:, 8 + j:9 + j], None, ALU.mult, None, (b_identb, b_dT), (b_Dsk,))
                for dr in range(2):
                    u = j * 2 + dr
                    head_unit(j, dr)
                    if u + 1 < 8:
                        gen_unit((u + 1) // 2, (u + 1) % 2)
                    tail_unit(j, dr)
                yacc(j)
            for ri, dst in ((0, o_sre), (1, o_sim)):
                for sq in range(2):
                    pb = sps()
                    tp(PS[pb][0:32, 0:128], FIN[:, ri * 64 + sq * 32:ri * 64 + (sq + 1) * 32], ident, (b_FIN, b_ident), (PSB[pb],))
                    cp('dve', rowst[0:32, (ri * 2 + sq) * 128:(ri * 2 + sq + 1) * 128], PS[pb][0:32, 0:128], (PSB[pb],), (b_rowst,))
                    dma('sp', dst[sq, :, :], rowst[0:32, (ri * 2 + sq) * 128:(ri * 2 + sq + 1) * 128], (b_rowst,), (), is_out=True)
        A.release(markB)
        AW.release(0)

        if stop_after == 14:
            S.finalize(nc, sems, dsems, None)
            return nc
        grow, b_grow = A.alloc("grow", 2 * 2 * D)
        mkg = A.mark()
        ones32, b_ones32 = A.alloc("ones32", 128)
        GG, b_GG = A.alloc("GG", 32)
        Dg = [A.alloc(f"Dg{i}", 128) for i in range(4)]
        memset('dve', ones32, 1.0, (b_ones32,))
        for g2_, (comp, gi_) in enumerate([(2, 1), (5, 3)]):
            tt('dve', V(GG, g2_ * 16, [(2, 8), (1, 2)]), V(modT, comp * 16, [(2, 8), (1, 2)]), V(ngT, gi_ * 8, [(1, 8), (0, 2)]),
               ALU.mult, (b_modT, b_ngT), (b_GG,))
        kk = 0
        for g2_ in range(2):
            for c in range(2):
                for half in range(2):
                    pb = next_ps()
                    for k4 in range(4):
                        k_ = half * 4 + k4
                        dg_t, dg_b = Dg[kk % 4]
                        kk += 1
                        ts('dve', dg_t, ident, GG[:, g2_ * 16 + k_ * 2 + c:g2_ * 16 + k_ * 2 + c + 1], None, ALU.mult, None,
                           (b_ident, b_GG), (dg_b,))
                        mm(PS[pb][:, k4 * 128:(k4 + 1) * 128], ones32, dg_t, True, True, (b_ones32, dg_b), (PSB[pb],))
                    cp(ev_eng(), grow[:, (g2_ * 2 + c) * D + half * 512:(g2_ * 2 + c) * D + half * 512 + 512], PS[pb], (PSB[pb],), (b_grow,))
        A.release(mkg)
        x1 = [A.alloc(f"x1_{i}", D) for i in range(8)]
        wfo = [A.alloc(f"wfo{j}", 1024, BF16) for j in range(22)]
        mark5 = A.mark()
        mixT, b_mixT = A.alloc("mixT", 8 * 1024, BF16)
        wo = [A.alloc(f"wo{c}", 1024, BF16) for c in range(8)]
        wglu = [A.alloc(f"wglu{c}", 512, BF16) for c in range(4)]
        bgluT, b_bgluT = A.alloc("bgluT", 4)
        sig = [A.alloc(f"sig{i}", 512, BF16) for i in range(2)]
        xst5 = [A.alloc(f"xst5_{i}", D) for i in range(2)]
        tmp5 = [A.alloc(f"tmp5_{i}", D) for i in range(2)]
        sm5 = [A.alloc(f"sm5_{i}", 8) for i in range(4)]
        dma('sp', bgluT, b_glu.rearrange("(j p) -> p j", p=128), (), (b_bgluT,), nc_ok=True)
        for c in range(4):
            load_cast(wglu[c][0], wglu[c][1], w_glu[c * 128:(c + 1) * 128, :], 512)
        for c in range(8):
            load_cast(wo[c][0], wo[c][1], w_o[c * 128:(c + 1) * 128, :], 1024)
        psrot['p'] = 0
        for tile_i in range(8):
            pb = next_tps()
            for h in range(4):
                tp(PSH[pb][:, h * 128:(h + 1) * 128], attnO[:, tile_i * 512 + h * 128:tile_i * 512 + (h + 1) * 128], identb,
                   (b_attnO, b_identb), (PSB[pb],))
            cp(ev_eng(), V(mixT, tile_i * 128, [(1024, 4), (1, 128)]), V(PSH[pb], 0, [(128, 4), (1, 128)]), (PSB[pb],), (b_mixT,))
        for blk in range(2):
            for jo in range(4):
                pb = next_ps()
                for ji in range(4):
                    mm(PS[pb], wglu[ji][0][:, jo * 128:(jo + 1) * 128], zT[:, ji * 1024 + blk * 512:ji * 1024 + (blk + 1) * 512],
                       ji == 0, ji == 3, (wglu[ji][1], b_zT), (PSB[pb],))
                sg_t, sg_b = sig[(blk * 4 + jo) % 2]
                act(sg_t, PS[pb], AF.Sigmoid, (PSB[pb], b_bgluT), (sg_b,), bias=bgluT[:, jo:jo + 1])
                tt('dve', mixT[:, (4 + jo) * 1024 + blk * 512:(4 + jo) * 1024 + (blk + 1) * 512],
                   zT[:, jo * 1024 + blk * 512:jo * 1024 + (blk + 1) * 512], sg_t, ALU.mult, (b_zT, sg_b), (b_mixT,))

        def branch_out(tile_i, pbs, gate_idx, xin, xin_b, xout, xout_b, k6):
            cidx = 0 if tile_i < 4 else 1
            sm, sm_b = sm5[k6 % 4]
            t5, t5_b = tmp5[k6 % 2]
            for half in range(2):
                act(t5[:, half * 512:(half + 1) * 512], PS[pbs[half]], AF.Square, (PSB[pbs[half]],), (t5_b, sm_b),
                    accum_out=sm[:, half:half + 1])
            tt('dve', sm[:, 2:3], sm[:, 0:1], sm[:, 1:2], ALU.add, (sm_b,), (sm_b,))
            act(sm[:, 3:4], sm[:, 2:3], AF.Sqrt, (sm_b, b_eps), (sm_b,), bias=epsv[:, 0:1], scale=1.0 / D)
            recip(sm[:, 4:5], sm[:, 3:4], (sm_b,), (sm_b,))
            g0 = (gate_idx * 2 + cidx) * D
            for half in range(2):
                stt('dve', t5[:, half * 512:(half + 1) * 512], PS[pbs[half]], sm[:, 4:5], grow[:, g0 + half * 512:g0 + (half + 1) * 512],
                    ALU.mult, ALU.mult, (PSB[pbs[half]], sm_b, b_grow), (t5_b,))
            tt('pool', xout, t5, xin, ALU.add, (t5_b, xin_b), (xout_b,))

        for tile_i in range(8):
            xs_t, xs_b = xst5[tile_i % 2]
            rows = xp[tile_i * 128:(tile_i + 1) * 128, :] if tile_i < 4 else xs[(tile_i - 4) * 128:(tile_i - 3) * 128, :]
            dma('sp', xs_t, rows, (), (xs_b,))
            pbs = [next_ps(), next_ps()]
            for half in range(2):
                for c in range(8):
                    mm(PS[pbs[half]], mixT[:, c * 1024 + tile_i * 128:c * 1024 + (tile_i + 1) * 128],
                       wo[c][0][:, half * 512:(half + 1) * 512], c == 0, c == 7, (b_mixT, wo[c][1]), (PSB[pbs[half]],))
            branch_out(tile_i, pbs, 0, xs_t, xs_b, x1[tile_i][0], x1[tile_i][1], tile_i)
        if 'x1' in dbg:
            for tile_i in range(8):
                dma('sp', dbg['x1'][tile_i * 128:(tile_i + 1) * 128, :], x1[tile_i][0], (x1[tile_i][1],), (), is_out=True)
        A.release(mark5)
        A.release(A.n, top=True)

        if stop_after == 16:
            S.finalize(nc, sems, dsems, None)
            return nc
        h2T, b_h2T = A.alloc("h2T", 8 * 1024, BF16)
        fT = [A.alloc(f"fT{j}", 1024, BF16) for j in range(22)]
        wfi = [AW.alloc(f"wfi{i}", 2 * 8 * 256, BF16) for i in range(2)]
        xnb7 = [AW.alloc(f"xnb7_{i}", D, BF16) for i in range(2)]
        junk7, b_junk7 = A.alloc("junk7", D, BF16)
        sm7 = [A.alloc(f"sm7_{i}", 8) for i in range(4)]
        sg7 = [A.alloc(f"sg7_{i}", 512, BF16) for i in range(2)]
        tmp7 = [A.alloc(f"tmp7_{i}", D) for i in range(1)]
        for tile_i in range(8):
            cidx = 0 if tile_i < 4 else 1
            xs_t, xs_b = x1[tile_i]
            xn_t, xn_b = xnb7[tile_i % 2]
            sm, sm_b = sm7[tile_i % 4]
            act(junk7, xs_t, AF.Square, (xs_b,), (b_junk7, sm_b), accum_out=sm[:, 0:1])
            act(sm[:, 1:2], sm[:, 0:1], AF.Sqrt, (sm_b, b_eps), (sm_b,), bias=epsv[:, 0:1], scale=1.0 / D)
            recip(sm[:, 2:3], sm[:, 1:2], (sm_b,), (sm_b,))
            ts('dve', xn_t, xs_t, sm[:, 2:3], None, ALU.mult, None, (xs_b, sm_b), (xn_b,))
            pb = next_tps()
            for kc in range(8):
                tp(PSH[pb][:, kc * 128:(kc + 1) * 128], xn_t[:, kc * 128:(kc + 1) * 128], identb, (xn_b, b_identb), (PSB[pb],))
            for kc in range(8):
                ts('dve', h2T[:, kc * 1024 + tile_i * 128:kc * 1024 + (tile_i + 1) * 128], PSH[pb][:, kc * 128:(kc + 1) * 128],
                   sc2[:, kc * 2 + cidx:kc * 2 + cidx + 1], modT[:, 48 + kc * 2 + cidx:48 + kc * 2 + cidx + 1], ALU.mult, ALU.add,
                   (PSB[pb], b_sc2, b_modT), (b_h2T,))
        wfi_src = w_ffn_in.rearrange("(kc p) f -> p kc f", p=128)
        for jp in range(11):
            wt, wb = wfi[jp % 2]
            for gu in range(2):
                for kh in range(2):
                    st_t, st_b = wst['bufs'][wst['n'] % 2]
                    wst['n'] += 1
                    c0 = gu * 2816 + jp * 256
                    dma('sp', V(st_t, 0, [(256, 4), (1, 256)]), wfi_src[:, kh * 4:(kh + 1) * 4, c0:c0 + 256], (), (st_b,))
                    cp('pool', wt[:, (gu * 8 + kh * 4) * 256:(gu * 8 + kh * 4 + 4) * 256], st_t, (st_b,), (wb,))
            for jw in (2 * jp, 2 * jp + 1):
                load_cast(wfo[jw][0], wfo[jw][1], w_ffn_out[jw * 128:(jw + 1) * 128, :], 1024)
            for jj in range(2):
                j = jp * 2 + jj
                for half in range(2):
                    pg, pu = next_ps(), next_ps()
                    for gu, pbank in ((0, pg), (1, pu)):
                        for kc in range(8):
                            mm(PS[pbank], wt[:, (gu * 8 + kc) * 256 + jj * 128:(gu * 8 + kc) * 256 + (jj + 1) * 128],
                               h2T[:, kc * 1024 + half * 512:kc * 1024 + (half + 1) * 512], kc == 0, kc == 7, (wb, b_h2T), (PSB[pbank],))
                    sg_t, sg_b = sg7[(j * 2 + half) % 2]
                    act(sg_t, PS[pg], AF.Silu, (PSB[pg],), (sg_b,))
                    tt('dve', fT[j][0][:, half * 512:(half + 1) * 512], PS[pu], sg_t, ALU.mult, (PSB[pu], sg_b), (fT[j][1],))
        for tile_i in range(8):
            pbs = [next_ps(), next_ps()]
            for half in range(2):
                for j in range(22):
                    mm(PS[pbs[half]], fT[j][0][:, tile_i * 128:(tile_i + 1) * 128], wfo[j][0][:, half * 512:(half + 1) * 512],
                       j == 0, j == 21, (fT[j][1], wfo[j][1]), (PSB[pbs[half]],))
            cidx = 0 if tile_i < 4 else 1
            sm, sm_b = sm7[tile_i % 4]
            t5, t5_b = tmp7[0]
            yo, yo_b = t5, t5_b
            for half in range(2):
                act(junk7[:, half * 512:(half + 1) * 512], PS[pbs[half]], AF.Square, (PSB[pbs[half]],), (b_junk7, sm_b),
                    accum_out=sm[:, half:half + 1])
            tt('dve', sm[:, 2:3], sm[:, 0:1], sm[:, 1:2], ALU.add, (sm_b,), (sm_b,))
            act(sm[:, 3:4], sm[:, 2:3], AF.Sqrt, (sm_b, b_eps), (sm_b,), bias=epsv[:, 0:1], scale=1.0 / D)
            recip(sm[:, 4:5], sm[:, 3:4], (sm_b,), (sm_b,))
            g0 = (1 * 2 + cidx) * D
            for half in range(2):
                stt('dve', t5[:, half * 512:(half + 1) * 512], PS[pbs[half]], sm[:, 4:5], grow[:, g0 + half * 512:g0 + (half + 1) * 512],
                    ALU.mult, ALU.mult, (PSB[pbs[half]], sm_b, b_grow), (t5_b,))
            tt('pool', yo, t5, x1[tile_i][0], ALU.add, (t5_b, x1[tile_i][1]), (yo_b,))
            dst = yp[tile_i * 128:(tile_i + 1) * 128, :] if tile_i < 4 else ys[(tile_i - 4) * 128:(tile_i - 3) * 128, :]
            dma('sp', dst, yo, (yo_b,), (), is_out=True)

        S.finalize(nc, sems, dsems, None)
    return nc


def _rope_fm(tokens):
    t = np.asarray(tokens)
    row = (t // 64).astype(np.float32)
    col = (t % 64).astype(np.float32)
    inv = (np.float32(10000.0) ** (-np.arange(0, 32, 2, dtype=np.float32) / np.float32(32))).astype(np.float32)
    ar = row[:, None] * inv
    ac = col[:, None] * inv
    ang = np.concatenate([ar, ar, ac, ac], -1)
    c = np.cos(ang).astype(np.float32).T
    s = np.sin(ang).astype(np.float32).T
    return np.ascontiguousarray(np.concatenate([c, c], 0)), np.ascontiguousarray(np.concatenate([s, s], 0))


def _consts():
    ident = np.eye(128, dtype=np.float32)
    R = np.zeros((128, 128), np.float32)
    for blk in range(2):
        o = blk * 64
        for f in range(16):
            R[o + f, o + f + 16] = -1.0
            R[o + 16 + f, o + f] = 1.0
            R[o + 32 + f, o + 48 + f] = -1.0
            R[o + 48 + f, o + 32 + f] = 1.0
    tab = np.zeros((128, 64), np.float32)
    tab[:, 0:17] = np.arange(17, dtype=np.float32)[None]
    tab[:, 17:49] = (16.0 * np.arange(32, dtype=np.float32))[None]
    return ident, np.ascontiguousarray(R.T), tab


def prep_core(inputs, j):
    f = lambda a: np.ascontiguousarray(np.asarray(a, dtype=np.float32))
    b, q = j // 4, j % 4
    order = [(q + i) % 4 for i in range(4)]
    xs = f(inputs['x_sample'])[b].reshape(4, 512, 1024)[order].reshape(2048, 1024)
    toks = np.concatenate([np.arange(512) + 512 * sg for sg in order])
    rc, rs = _rope_fm(toks)
    ident, rrotT, tab = _consts()
    flags = np.zeros((128, 32), np.float32)
    for i in range(3):
        sg = (q + 1 + i) % 4
        flags[:, i] = 1.0 if sg < q else 0.0
        flags[:, 3 + i] = 1.0 if sg > q else 0.0
    injF = (3 - q) if q >= 1 else 3
    injB = (2 - q) if q <= 2 else 3
    flags[:, 6 + injF] = 1.0
    flags[:, 10 + injB] = 1.0
    d = {
        'xp': f(inputs['x_prompt'])[2 * j:2 * j + 2].reshape(512, 1024),
        'xs': xs, 'ropec': rc, 'ropes': rs,
        'ck': f(inputs['cache_k'])[b, 0].reshape(256, 512),
        'cv': f(inputs['cache_v'])[b, 0].reshape(256, 512),
        'h0re': f(inputs['state_ssm_re'])[b, 0].reshape(32, 128),
        'h0im': f(inputs['state_ssm_im'])[b, 0].reshape(32, 128),
        'cond2': np.stack([f(inputs['c_ctx']), f(inputs['c'])[b]], 0),
        'w_mod': f(inputs['w_mod'])[0], 'b_mod': f(inputs['b_mod'])[0], 'norm_g': f(inputs['norm_g'])[0],
        'w_in': f(inputs['w_in'])[0], 'lam_params': f(inputs['lam_params'])[0], 'subln_g': f(inputs['subln_g'])[0],
        's_lre': f(inputs['ssm_lambda_re'])[0].reshape(32, 128), 's_lim': f(inputs['ssm_lambda_im'])[0].reshape(32, 128),
        's_lstep': f(inputs['ssm_log_step'])[0].reshape(32, 2),
        's_bre': f(inputs['ssm_b_re'])[0], 's_bim': f(inputs['ssm_b_im'])[0],
        's_cre': f(inputs['ssm_c_re'])[0], 's_cim': f(inputs['ssm_c_im'])[0],
        's_d': f(inputs['ssm_d'])[0], 'w_glu': f(inputs['w_glu'])[0], 'b_glu': f(inputs['b_glu'])[0],
        'w_o': f(inputs['w_o'])[0], 'w_ffn_in': f(inputs['w_ffn_in'])[0], 'w_ffn_out': f(inputs['w_ffn_out'])[0],
        'c_ident': ident, 'c_rrot': rrotT, 'c_tab': tab, 'c_flags': flags,
    }
    return {k: np.ascontiguousarray(v) for k, v in d.items()}


_NC_CACHE = {}


def kernel(**inputs):
    if 'nc' not in _NC_CACHE:
        import os
        _NC_CACHE['nc'] = build_program(stop_after=int(os.environ.get('KSTOP', '99')))
    nc = _NC_CACHE['nc']
    maps = [prep_core(inputs, j) for j in range(8)]
    res = run_bass_kernel_spmd(nc, maps, core_ids=list(range(8))).results
    y_prompt = np.zeros((16, 256, 1024), np.float32)
    y_sample = np.zeros((2, 2048, 1024), np.float32)
    nk = np.zeros((16, 1, 256, 8, 64), np.float32)
    nv = np.zeros((16, 1, 256, 4, 128), np.float32)
    sre = np.zeros((16, 1, 2, 32, 64), np.float32)
    sim_ = np.zeros((16, 1, 2, 32, 64), np.float32)
    for j in range(8):
        r = {k_: (res[j][k_] if k_ in res[j] else 0.0) for k_ in ('yp', 'ys', 'o_nk', 'o_nv', 'o_sre', 'o_sim')}
        b, q = j // 4, j % 4
        if not hasattr(r['yp'], 'shape') or not hasattr(r['o_sre'], 'shape'):
            continue
        y_prompt[2 * j:2 * j + 2] = np.asarray(r['yp']).reshape(2, 256, 1024)
        y_sample[b, q * 512:(q + 1) * 512] = np.asarray(r['ys'])
        nk[2 * j:2 * j + 2, 0] = np.asarray(r['o_nk']).reshape(2, 256, 8, 64)
        nv[2 * j:2 * j + 2, 0] = np.asarray(r['o_nv']).reshape(2, 256, 4, 128)
        sre[2 * j:2 * j + 2, 0] = np.asarray(r['o_sre']).reshape(2, 2, 32, 64)
        sim_[2 * j:2 * j + 2, 0] = np.asarray(r['o_sim']).reshape(2, 2, 32, 64)
    return (y_prompt, y_sample, nk, nv, sre, sim_)
```
